# Optimizing a Trainium2 kernel written in Bass

```python
import math
import jax
import jax.numpy as jnp
from jax import lax
import numpy as np

D_MODEL = 2048
BATCH = 2
SEQ = 4096
DEPTH = 4

N_MIXERS = 3
D_INNER = D_MODEL
CHUNK = 128
SG_GROUPS = 16
SG_GROUP_DIM = D_INNER // SG_GROUPS
SG_COLS = 3 * D_INNER
HEAD_DIM = 64
SWA_HEADS = D_INNER // HEAD_DIM
SWA_KV_HEADS = SWA_HEADS // 8
SWA_REP = SWA_HEADS // SWA_KV_HEADS
WINDOW = 128
BLOCK = 128
ROPE_THETA = 10000.0
SWA_COLS = 2 * D_INNER + 2 * SWA_KV_HEADS * HEAD_DIM
RWKV_HEAD_DIM = 64
RWKV_HEADS = D_INNER // RWKV_HEAD_DIM
DECAY_LORA = 96
AAA_LORA = 96
RWKV_COLS = 4 * D_INNER + DECAY_LORA + AAA_LORA
DECAY_SCALE = math.exp(-0.5)
GN_EPS = 64e-5
RMS_EPS = 1e-6
LN_EPS = 1e-5
N_A = (DEPTH + 2) // 3
N_B = (DEPTH + 1) // 3
N_C = DEPTH // 3

kernel_name = 'hybrid_sgmlp_swa_rwkv7_adaln'


def rms_norm(x, g):
    xf = x.astype(jnp.float32)
    y = xf * lax.rsqrt(jnp.mean(xf * xf, axis=-1, keepdims=True) + RMS_EPS)
    return (y * g.astype(jnp.float32)).astype(x.dtype)


def token_shift(t):
    return jnp.concatenate([jnp.zeros_like(t[:, :1]), t[:, :-1]], axis=1)


def rope(x, positions):
    half = HEAD_DIM // 2
    inv_freq = ROPE_THETA ** (-jnp.arange(half, dtype=jnp.float32) / half)
    ang = positions.astype(jnp.float32)[..., None] * inv_freq
    cos = jnp.cos(ang)[:, :, None, :]
    sin = jnp.sin(ang)[:, :, None, :]
    xf = x.astype(jnp.float32)
    x1, x2 = xf[..., :half], xf[..., half:]
    return jnp.concatenate([x1 * cos - x2 * sin, x2 * cos + x1 * sin], axis=-1).astype(x.dtype)


def chunked_spatial_gating(p, ln_g, ln_b, w_s, b_s):
    B, T, _ = p.shape
    u, v, z = jnp.split(p, 3, axis=-1)
    u = jax.nn.gelu(u)
    vf = jax.nn.gelu(v).astype(jnp.float32)
    mean = jnp.mean(vf, axis=-1, keepdims=True)
    var = jnp.mean(jnp.square(vf - mean), axis=-1, keepdims=True)
    v = ((vf - mean) * lax.rsqrt(var + LN_EPS) * ln_g.astype(jnp.float32) + ln_b.astype(jnp.float32)).astype(p.dtype)
    nc = T // CHUNK
    v = v.reshape(B, nc, CHUNK, SG_GROUPS, SG_GROUP_DIM)
    causal = jnp.tril(jnp.ones((CHUNK, CHUNK), dtype=bool))
    w = jnp.where(causal[None], w_s, jnp.zeros_like(w_s))
    f = jnp.einsum('gts,bnsgc->bntgc', w, v) + b_s.T[:, :, None]
    f = f.reshape(B, T, D_INNER)
    return u * f * jax.nn.silu(z)


def sliding_window_attention(p, positions, sinks):
    B, T, _ = p.shape
    kvw = SWA_KV_HEADS * HEAD_DIM
    q, k, v, z = jnp.split(p, [D_INNER, D_INNER + kvw, D_INNER + 2 * kvw], axis=-1)
    q = rope(q.reshape(B, T, SWA_HEADS, HEAD_DIM), positions)
    k = rope(k.reshape(B, T, SWA_KV_HEADS, HEAD_DIM), positions)
    v = v.reshape(B, T, SWA_KV_HEADS, HEAD_DIM)
    nb = T // BLOCK
    qb = q.reshape(B, nb, BLOCK, SWA_KV_HEADS, SWA_REP, HEAD_DIM)

    def with_prev(t):
        tb = t.reshape(B, nb, BLOCK, SWA_KV_HEADS, HEAD_DIM)
        prev = jnp.concatenate([jnp.zeros_like(tb[:, :1]), tb[:, :-1]], axis=1)
        return jnp.concatenate([prev, tb], axis=2)

    kb, vb = with_prev(k), with_prev(v)
    s = jnp.einsum('bnqgrd,bnkgd->bngrqk', qb, kb,
                   preferred_element_type=jnp.float32) * (HEAD_DIM ** -0.5)
    qi = jnp.arange(BLOCK)[:, None]
    kj = jnp.arange(2 * BLOCK)[None, :]
    rel = qi + BLOCK - kj
    band = (rel >= 0) & (rel < WINDOW)
    key_pos = jnp.arange(nb)[:, None] * BLOCK + jnp.arange(2 * BLOCK)[None, :] - BLOCK
    mask = band[None] & (key_pos >= 0)[:, None, :]
    s = jnp.where(mask[None, :, None, None], s, -jnp.inf)
    sink = sinks.astype(jnp.float32).reshape(SWA_KV_HEADS, SWA_REP)[None, None, :, :, None, None]
    m = jnp.maximum(jnp.max(s, axis=-1, keepdims=True), sink)
    e = jnp.exp(s - m)
    denom = jnp.sum(e, axis=-1, keepdims=True) + jnp.exp(sink - m)
    prob = (e / denom).astype(p.dtype)
    o = jnp.einsum('bngrqk,bnkgd->bnqgrd', prob, vb).reshape(B, T, D_INNER)
    return o * jax.nn.silu(z)


def rwkv7_time_mix(p, mu, w0, w_lora, a0, a_lora, k_k, k_a, r_k, gn_g, gn_b):
    B, T, _ = p.shape
    H, N = RWKV_HEADS, RWKV_HEAD_DIM
    p = p + (token_shift(p) - p) * mu
    r, k, v, z, dw, da = jnp.split(p, [D_INNER, 2 * D_INNER, 3 * D_INNER, 4 * D_INNER,
                                       4 * D_INNER + DECAY_LORA], axis=-1)
    decay = jnp.exp(-DECAY_SCALE * jax.nn.sigmoid((w0 + jnp.tanh(dw) @ w_lora).astype(jnp.float32)))
    a = jax.nn.sigmoid((a0 + da @ a_lora).astype(jnp.float32))

    def heads(t):
        return t.astype(jnp.float32).reshape(B, T, H, N)

    r, k, v, decay, a = heads(r), heads(k), heads(v), heads(decay), heads(a)
    kk = k * k_k.astype(jnp.float32).reshape(H, N)
    kk = kk / jnp.maximum(jnp.sqrt(jnp.sum(kk * kk, axis=-1, keepdims=True)), 1e-12)
    k = k * (1.0 + (a - 1.0) * k_a.astype(jnp.float32).reshape(H, N))
    b_vec = kk * a

    def step(S, inp):
        r_t, w_t, k_t, v_t, kk_t, b_t = inp
        sa = jnp.einsum('bhvk,bhk->bhv', S, kk_t)
        S = S * w_t[:, :, None, :] - sa[..., None] * b_t[:, :, None, :] + v_t[..., None] * k_t[:, :, None, :]
        y = jnp.einsum('bhvk,bhk->bhv', S, r_t)
        return S, y

    xs = tuple(jnp.moveaxis(t, 1, 0) for t in (r, decay, k, v, kk, b_vec))
    S0 = jnp.zeros((B, H, N, N), jnp.float32)
    _, y = lax.scan(step, S0, xs)
    y = jnp.moveaxis(y, 0, 1)
    mean = jnp.mean(y, axis=-1, keepdims=True)
    var = jnp.mean(jnp.square(y - mean), axis=-1, keepdims=True)
    y = (y - mean) * lax.rsqrt(var + GN_EPS) * gn_g.astype(jnp.float32).reshape(H, N) \
        + gn_b.astype(jnp.float32).reshape(H, N)
    y = y + jnp.sum(r * k * r_k.astype(jnp.float32), axis=-1, keepdims=True) * v
    y = y.reshape(B, T, D_INNER).astype(p.dtype)
    return y * jax.nn.silu(z)


def setup_inputs(seed: int = 0) -> dict:
    key = jax.random.key(seed)
    keys = jax.random.split(key, 32)

    def nrm(i, shape, scale):
        return jax.random.normal(keys[i], shape, jnp.float32) * scale

    positions = jax.random.randint(keys[2], (BATCH, 1), 0, 1024, dtype=jnp.int32) \
        + jnp.arange(SEQ, dtype=jnp.int32)[None, :]
    return {
        'x': nrm(0, (BATCH, SEQ, D_MODEL), 1.0),
        'c': nrm(1, (BATCH, D_MODEL), 1.0),
        'positions': positions,
        'norm_g': 1.0 + nrm(3, (DEPTH, D_MODEL), 0.1),
        'mod_w': nrm(4, (DEPTH, D_MODEL, 3 * D_MODEL), 0.5 * D_MODEL ** -0.5),
        'mod_b': nrm(5, (DEPTH, 3 * D_MODEL), 0.02),
        'final_norm_g': 1.0 + nrm(6, (D_MODEL,), 0.1),
        'sg_w_in': nrm(7, (N_A, D_MODEL, SG_COLS), D_MODEL ** -0.5),
        'sg_w_out': nrm(8, (N_A, D_INNER, D_MODEL), D_INNER ** -0.5),
        'sg_ln_g': 1.0 + nrm(9, (N_A, D_INNER), 0.1),
        'sg_ln_b': nrm(10, (N_A, D_INNER), 0.02),
        'sg_w_spatial': nrm(11, (N_A, SG_GROUPS, CHUNK, CHUNK), CHUNK ** -0.5),
        'sg_b_spatial': 1.0 + nrm(12, (N_A, SG_GROUPS, CHUNK), 0.1),
        'swa_w_in': nrm(13, (N_B, D_MODEL, SWA_COLS), D_MODEL ** -0.5),
        'swa_w_out': nrm(14, (N_B, D_INNER, D_MODEL), D_INNER ** -0.5),
        'swa_sinks': nrm(15, (N_B, SWA_HEADS), 1.0),
        'rwkv_w_in': nrm(16, (N_C, D_MODEL, RWKV_COLS), D_MODEL ** -0.5),
        'rwkv_w_out': nrm(17, (N_C, D_INNER, D_MODEL), D_INNER ** -0.5),
        'rwkv_mu': jax.random.uniform(keys[18], (N_C, RWKV_COLS), jnp.float32),
        'rwkv_w0': jax.random.uniform(keys[19], (N_C, D_INNER), jnp.float32, -4.0, 1.0),
        'rwkv_w_lora': nrm(20, (N_C, DECAY_LORA, D_INNER), DECAY_LORA ** -0.5),
        'rwkv_a0': nrm(21, (N_C, D_INNER), 0.5),
        'rwkv_a_lora': nrm(22, (N_C, AAA_LORA, D_INNER), 0.5 * AAA_LORA ** -0.5),
        'rwkv_k_k': 0.85 + nrm(23, (N_C, D_INNER), 0.1),
        'rwkv_k_a': 1.0 + nrm(24, (N_C, D_INNER), 0.1),
        'rwkv_r_k': nrm(25, (N_C, RWKV_HEADS, RWKV_HEAD_DIM), 0.1),
        'rwkv_gn_g': 1.0 + nrm(26, (N_C, D_INNER), 0.1),
        'rwkv_gn_b': nrm(27, (N_C, D_INNER), 0.02),
    }


def reference(x, c, positions, norm_g, mod_w, mod_b, final_norm_g,
              sg_w_in, sg_w_out, sg_ln_g, sg_ln_b, sg_w_spatial, sg_b_spatial,
              swa_w_in, swa_w_out, swa_sinks,
              rwkv_w_in, rwkv_w_out, rwkv_mu, rwkv_w0, rwkv_w_lora, rwkv_a0, rwkv_a_lora,
              rwkv_k_k, rwkv_k_a, rwkv_r_k, rwkv_gn_g, rwkv_gn_b):
    cond = jax.nn.silu(c)
    for i in range(DEPTH):
        kind, j = i % N_MIXERS, i // N_MIXERS
        mod = (cond @ mod_w[i] + mod_b[i])[:, None, :]
        shift, scale, gate = jnp.split(mod, 3, axis=-1)
        h = rms_norm(x, norm_g[i]) * (1.0 + scale) + shift
        if kind == 0:
            y = chunked_spatial_gating(h @ sg_w_in[j], sg_ln_g[j], sg_ln_b[j],
                                       sg_w_spatial[j], sg_b_spatial[j]) @ sg_w_out[j]
        elif kind == 1:
            y = sliding_window_attention(h @ swa_w_in[j], positions, swa_sinks[j]) @ swa_w_out[j]
        else:
            y = rwkv7_time_mix(h @ rwkv_w_in[j], rwkv_mu[j], rwkv_w0[j], rwkv_w_lora[j],
                               rwkv_a0[j], rwkv_a_lora[j], rwkv_k_k[j], rwkv_k_a[j],
                               rwkv_r_k[j], rwkv_gn_g[j], rwkv_gn_b[j]) @ rwkv_w_out[j]
        x = x + gate * y
    return rms_norm(x, final_norm_g)
```

```python
import contextlib
import os
import numpy as np
import concourse.bass as bass
import concourse.mybir as mybir
from concourse.bass_utils import run_bass_kernel_spmd

F32 = mybir.dt.float32
BF16 = mybir.dt.bfloat16
I32 = mybir.dt.int32
AF = mybir.ActivationFunctionType
ALU = mybir.AluOpType
AX = mybir.AxisListType

ENGS = ("pe", "act", "dve", "pool", "sp")
N_DMA_SEMS = 12
SAME_ENG_SYNC = True

D = 2048
KC = 16
OWN = 1024
HALO = 128
NT = OWN + HALO
NCORES = 8
TILES = [(0, 128), (128, 512), (640, 512)]
N_LAYERS = 4
RMS_EPS = 1e-6
LN_EPS = 1e-5
GN_EPS = 64e-5
DECAY_SCALE = float(np.exp(-0.5))


class Res:
    __slots__ = ("name", "w", "r")

    def __init__(self, name=""):
        self.name = name
        self.w = None
        self.r = []


class Prog:
    def __init__(self, nc):
        self.nc = nc
        self.ops = {e: [] for e in ENGS}
        self.seq = {e: 0 for e in ENGS}
        self.known = {e: {} for e in ENGS}
        self.tok_know = {}
        self.dma_rr = {e: 0 for e in ENGS}
        self.dma_val = {}
        self.dma_last = {}
        self.last_tok = {}

    def _need(self, eng, toks):
        kn = self.known[eng]
        waits = {}
        for t in toks:
            if t is None:
                continue
            key, val = t
            if kn.get(key, 0) >= val:
                continue
            if waits.get(key, 0) < val:
                waits[key] = val
        out = []
        for key, val in waits.items():
            if kn.get(key, 0) >= val:
                continue
            out.append((key, val))
            tk = self.tok_know.get((key, val))
            if tk is not None:
                for k2, v2 in tk.items():
                    if kn.get(k2, 0) < v2:
                        kn[k2] = v2
            kn[key] = val
        return out

    def _deps(self, reads, writes):
        toks = []
        for r in reads:
            toks.append(r.w)
        for w in writes:
            toks.append(w.w)
            toks.extend(w.r)
        return toks

    def _commit(self, tok, reads, writes):
        for r in reads:
            r.r.append(tok)
            if len(r.r) > 64:
                r.r = r.r[-64:] if False else r.r
        for w in writes:
            w.w = tok
            w.r = []

    def op(self, eng, fn, reads=(), writes=(), extra=()):
        toks = self._deps(reads, writes) + list(extra)
        waits = self._need(eng, toks)
        self.seq[eng] += 1
        tok = (eng, self.seq[eng])
        if eng == "pe" or not SAME_ENG_SYNC:
            self.known[eng][eng] = self.seq[eng]
        tk = dict(self.known[eng])
        tk[eng] = self.seq[eng]
        self.tok_know[tok] = tk
        self.ops[eng].append((waits, fn, None))
        self._commit(tok, reads, writes)
        self.last_tok[eng] = tok
        return tok

    def dma(self, q, fn, reads=(), writes=(), extra=()):
        i = self.dma_rr[q]
        self.dma_rr[q] = (i + 1) % N_DMA_SEMS
        key = ("d", q, i)
        toks = [self.dma_last.get(key)] + self._deps(reads, writes) + list(extra)
        waits = self._need(q, toks)
        val = self.dma_val.get(key, 0) + 16
        self.dma_val[key] = val
        tok = (key, val)
        self.dma_last[key] = tok
        self.tok_know[tok] = dict(self.known[q])
        self.ops[q].append((waits, fn, key))
        self._commit(tok, reads, writes)
        return tok

    def wait_all(self, eng, toks):
        waits = self._need(eng, toks)
        self.ops[eng].append((waits, None, None))

    def emit(self):
        nc = self.nc
        with contextlib.ExitStack() as st:
            esem = {e: st.enter_context(nc.semaphore("s_" + e)) for e in ENGS}
            dsem = {}
            for q in ENGS:
                if any(o[2] is not None for o in self.ops[q]):
                    for i in range(N_DMA_SEMS):
                        dsem[("d", q, i)] = st.enter_context(nc.semaphore("d_%s_%d" % (q, i)))
            block = st.enter_context(nc.Block())

            def semof(key):
                return esem[key] if isinstance(key, str) else dsem[key]

            def run(eng_name, e):
                for waits, fn, dkey in self.ops[eng_name]:
                    for key, val in waits:
                        e.wait_ge(semof(key), val)
                    if fn is None:
                        continue
                    ins = fn(e)
                    if dkey is None:
                        ins.then_inc(esem[eng_name], 1)
                    else:
                        ins.then_inc(dsem[dkey], 16)

            @block.tensor
            def _(e):
                run("pe", e)

            @block.scalar
            def _(e):
                run("act", e)

            @block.vector
            def _(e):
                run("dve", e)

            @block.gpsimd
            def _(e):
                run("pool", e)

            @block.sync
            def _(e):
                run("sp", e)


def _blocks(W):
    nb = W.shape[1] // 128
    return np.ascontiguousarray(W.reshape(KC, 128, nb, 128).transpose(2, 1, 0, 3)).reshape(nb, 128, D)


def _pv(v):
    return np.ascontiguousarray(np.asarray(v, np.float32).reshape(KC, 128).T)


class VecPack:
    def __init__(self):
        self.cols = []
        self.off = {}
        self.n = 0

    def add(self, name, arr):
        arr = np.asarray(arr, np.float32)
        assert arr.shape[0] == 128
        self.off[name] = (self.n, arr.shape[1])
        self.cols.append(arr)
        self.n += arr.shape[1]

    def build(self):
        return np.ascontiguousarray(np.concatenate(self.cols, axis=1))


def layer_kind(i):
    return i % 3, i // 3


def build_wstream(inp, nlayers):
    blks = []
    for i in range(nlayers):
        kind, j = layer_kind(i)
        blks.append(_blocks(inp["mod_w"][i]))
        if kind == 0:
            W = inp["sg_w_in"][j]
            blks.append(_blocks(W[:, 2048:4096]))
            u = _blocks(W[:, 0:2048])
            z = _blocks(W[:, 4096:6144])
            blks.append(np.stack([u, z], axis=1).reshape(32, 128, D))
            blks.append(_blocks(inp["sg_w_out"][j]))
        elif kind == 1:
            W = inp["swa_w_in"][j]
            blks.append(_blocks(W[:, 2048:2560]))
            q = _blocks(W[:, 0:2048])
            z = _blocks(W[:, 2560:4608])
            blks.append(np.stack([q, z], axis=1).reshape(32, 128, D))
            blks.append(_blocks(inp["swa_w_out"][j]))
        else:
            W = inp["rwkv_w_in"][j]
            lo = np.zeros((D, 256), np.float32)
            lo[:, 0:96] = W[:, 8192:8288]
            lo[:, 128:224] = W[:, 8288:8384]
            blks.append(_blocks(lo))
            r = _blocks(W[:, 0:2048]); k = _blocks(W[:, 2048:4096]); v = _blocks(W[:, 4096:6144]); z = _blocks(W[:, 6144:8192])
            blks.append(np.stack([r, k, v, z], axis=1).reshape(64, 128, D))
            blks.append(_blocks(inp["rwkv_w_out"][j]))
    a = np.concatenate(blks, axis=0)
    assert a.shape[0] % 2 == 0
    return np.ascontiguousarray(a.reshape(a.shape[0] // 2, 2, 128, D).transpose(0, 2, 1, 3)).reshape(a.shape[0] // 2, 128, 2 * D)


class Builder:
    def __init__(self, nlayers, nslots_total, vec_off, nvec, dbg):
        self.nlayers = nlayers
        self.vec_off = vec_off
        self.dbg = dbg
        nc = self.nc = bass.Bass("TRN2", target_bir_lowering=False)
        self.P = Prog(nc)
        self.d_xT = nc.dram_tensor("xT", [D, NT], F32, kind="ExternalInput").ap()
        self.d_w = nc.dram_tensor("wst", [nslots_total, 128, 2 * D], F32, kind="ExternalInput").ap()
        self.d_vec = nc.dram_tensor("vecs", [128, nvec], F32, kind="ExternalInput").ap()
        self.d_cm = nc.dram_tensor("cmat", [128, 8, 128], F32, kind="ExternalInput").ap()
        self.d_sgw = nc.dram_tensor("sgw", [2, 128, 16 * 128], F32, kind="ExternalInput").ap()
        self.d_sgb = nc.dram_tensor("sgb", [2, 16 * 128], F32, kind="ExternalInput").ap()
        self.d_pos = nc.dram_tensor("pos", [NT], I32, kind="ExternalInput").ap()
        self.d_msk = nc.dram_tensor("amask", [128, 3, 256], F32, kind="ExternalInput").ap()
        self.d_wlora = nc.dram_tensor("wlora", [2, 96, D], F32, kind="ExternalInput").ap()
        self.d_pmt = nc.dram_tensor("pmt", [3, 16, 128, 128], F32, kind="ExternalInput").ap()
        self.d_pn = nc.dram_tensor("pn", [3, 16, 128, 64], F32, kind="ExternalInput").ap()
        self.d_hout = nc.dram_tensor("hout", [16, 128, 128], F32, kind="ExternalOutput").ap()
        self.hout_toks = []
        self.dumps = {}
        self.d_out = nc.dram_tensor("outT", [D, OWN], F32, kind="ExternalOutput").ap()
        if dbg:
            self.d_dbg = nc.dram_tensor("dbgT", [D, NT], F32, kind="ExternalOutput").ap()
        self.nvec = nvec
        self.wslot = 0

    def dump(self, name, ap, res):
        if not self.dbg or name in self.dumps:
            return
        shp = [int(x) for x in ap.shape]
        d = self.nc.dram_tensor("dump_" + name, shp, ap.dtype, kind="ExternalOutput").ap()
        self.dumps[name] = d
        self.hout_toks.append(self.P.dma("sp", lambda e: e.dma_start(out=d, in_=ap), reads=list(res)))

    def V(self, name, c0=0, n=None):
        off, w = self.vec_off[name]
        if n is None:
            n = w - c0
        return self.vecs[:, off + c0: off + c0 + n]

    def next_w(self):
        P = self.P
        s = self.wslot
        self.wslot += 1
        r = s % len(self.wring)
        t, res = self.wring[r], self.wres[r]
        src = self.d_w[s]
        P.dma("pool", lambda e, t=t, src=src: e.dma_start(out=t[:], in_=src), writes=[res])
        return t[:].rearrange("p (b f) -> p b f", b=2), res

    def build(self):
        nc, P = self.nc, self.P
        with contextlib.ExitStack() as st:
            sb = lambda n, s, d: st.enter_context(nc.sbuf_tensor(n, s, d))
            ps = lambda n, s, d: st.enter_context(nc.psum_tensor(n, s, d))
            self.xT = sb("xT_sb", [128, KC, NT], F32)
            self.hT = sb("hT_sb", [128, KC, NT], BF16)
            self.big = sb("big_sb", [128, KC * NT], BF16)
            self.wring = [sb("wring%d" % i, [128, 2 * D], BF16) for i in range(2)]
            self.wres = [Res("wring%d" % i) for i in range(2)]
            self.vecs = sb("vecs_sb", [128, self.nvec], F32)
            self.cm = sb("cm_sb", [128, 8, 128], BF16)
            self.rstd = sb("rstd_sb", [128, NT], F32)
            self.sq = [sb("sq%d" % i, [128, 512], BF16) for i in range(2)]
            self.tmpf = [sb("tmpf%d" % i, [128, 512], F32) for i in range(3)]
            self.modv = sb("modv_sb", [128, 64], F32)
            self.condT = sb("condT_sb", [128, KC], BF16)
            self.small = sb("small_sb", [128, 64], F32)
            self.scr = sb("scr_sb", [128, 7824], F32)
            self.pb = [ps("pb%d" % i, [128, 512], F32) for i in range(8)]
            self.r_x = [[Res("x%d_%d" % (k, t)) for t in range(3)] for k in range(KC)]
            self.r_h = [[Res("h%d_%d" % (k, t)) for t in range(3)] for k in range(KC)]
            self.r_big = [[Res("big%d_%d" % (k, t)) for t in range(3)] for k in range(KC)]
            self.r_pb = [Res("pb%d" % i) for i in range(8)]
            self.r_vecs = Res("vecs"); self.r_cm = Res("cm"); self.r_rstd = [Res("rstd%d" % t) for t in range(3)]
            self.r_sq = [Res("sq0"), Res("sq1")]; self.r_tmpf = [Res("tf%d" % i) for i in range(3)]
            self.r_modv = Res("modv"); self.r_cond = Res("cond"); self.r_small = Res("small")
            self.r_scr = Res("scr")

            P.dma("sp", lambda e: e.dma_start(out=self.vecs[:], in_=self.d_vec), writes=[self.r_vecs])
            P.dma("pool", lambda e: e.dma_start(out=self.cm[:], in_=self.d_cm), writes=[self.r_cm])
            for k in range(KC):
                for ti, (ts, tn) in enumerate(TILES):
                    P.dma("sp", lambda e, k=k, ts=ts, tn=tn: e.dma_start(out=self.xT[:, k, ts:ts + tn], in_=self.d_xT[k * 128:(k + 1) * 128, ts:ts + tn]),
                          writes=[self.r_x[k][ti]])
            P.op("act", lambda e: e.activation(out=self.condT[:], in_=self.V("c"), func=AF.Silu), reads=[self.r_vecs], writes=[self.r_cond])

            for i in range(self.nlayers):
                kind, j = layer_kind(i)
                tiles = [0, 1, 2] if i < 2 else [1, 2]
                self.mod_and_norm(i, [0, 1, 2] if i <= 2 else tiles)
                if kind == 0:
                    self.sg_layer(i, j, tiles)
                elif kind == 1:
                    self.swa_layer(i, j, tiles)
                else:
                    self.rwkv_layer(i, j, tiles)
            outs = []
            if self.dbg:
                for k in range(KC):
                    for ti, (ts, tn) in enumerate(TILES):
                        outs.append(P.dma("sp", lambda e, k=k, ts=ts, tn=tn: e.dma_start(out=self.d_dbg[k * 128:(k + 1) * 128, ts:ts + tn], in_=self.xT[:, k, ts:ts + tn]),
                                          reads=[self.r_x[k][ti]]))
            outs += self.final_norm()
            outs += self.hout_toks
            P.wait_all("sp", outs)
            P.emit()
        return nc

    def mod_and_norm(self, i, tiles):
        P = self.P
        pmod = self.pb[7]
        for s in range(24):
            w, wres = self.next_w()
            for b in range(2):
                jb = 2 * s + b
                for kc in range(KC):
                    P.op("pe", lambda e, w=w, b=b, kc=kc, jb=jb: e.matmul(pmod[:, jb:jb + 1], w[:, b, kc * 128:(kc + 1) * 128], self.condT[:, kc:kc + 1],
                                                                          start=(kc == 0), stop=(kc == KC - 1)),
                         reads=[wres, self.r_cond], writes=[self.r_pb[7]])
        mv = self.modv
        P.op("dve", lambda e: e.tensor_tensor(out=mv[:, 0:48], in0=pmod[:, 0:48], in1=self.V("mod_b%d" % i), op=ALU.add),
             reads=[self.r_pb[7], self.r_vecs], writes=[self.r_modv])
        P.op("dve", lambda e: e.scalar_tensor_tensor(out=mv[:, 48:64], in0=mv[:, 16:32], scalar=1.0, in1=self.V("norm_g%d" % i), op0=ALU.add, op1=ALU.mult),
             reads=[self.r_modv, self.r_vecs], writes=[self.r_modv])
        self.rms_to_h(tiles, lambda kc: mv[:, 48 + kc:49 + kc], lambda kc: mv[:, kc:kc + 1], [self.r_modv])

    def rms_stats(self, tiles):
        P = self.P
        for ti in tiles:
            ts, tn = TILES[ti]
            pst = self.pb[6]
            for kc in range(KC):
                q = kc % 2
                P.op("act", lambda e, kc=kc, q=q, ts=ts, tn=tn: e.activation(out=self.sq[q][:, :tn], in_=self.xT[:, kc, ts:ts + tn], func=AF.Square),
                     reads=[self.r_x[kc][ti]], writes=[self.r_sq[q]])
                P.op("pe", lambda e, kc=kc, q=q, tn=tn: e.matmul(pst[:, :tn], self.cm[:, 1, :], self.sq[q][:, :tn], start=(kc == 0), stop=(kc == KC - 1)),
                     reads=[self.r_sq[q], self.r_cm], writes=[self.r_pb[6]])
            P.op("act", lambda e, ts=ts, tn=tn: e.activation(out=self.rstd[:, ts:ts + tn], in_=pst[:, :tn], func=AF.Sqrt, bias=self.V("eps_rms"), scale=1.0 / D),
                 reads=[self.r_pb[6], self.r_vecs], writes=[self.r_rstd[ti]])
            P.op("dve", lambda e, ts=ts, tn=tn: e.reciprocal(out=self.rstd[:, ts:ts + tn], in_=self.rstd[:, ts:ts + tn]),
                 reads=[self.r_rstd[ti]], writes=[self.r_rstd[ti]])

    def rms_to_h(self, tiles, A, Bsh, extra_reads):
        P = self.P
        self.rms_stats(tiles)
        for ti in tiles:
            ts, tn = TILES[ti]
            for kc in range(KC):
                q = kc % 3
                P.op("dve", lambda e, kc=kc, q=q, ts=ts, tn=tn: e.scalar_tensor_tensor(out=self.tmpf[q][:, :tn], in0=self.xT[:, kc, ts:ts + tn], scalar=A(kc),
                                                                                     in1=self.rstd[:, ts:ts + tn], op0=ALU.mult, op1=ALU.mult),
                     reads=[self.r_x[kc][ti], self.r_rstd[ti]] + extra_reads, writes=[self.r_tmpf[q]])
                P.op("act", lambda e, kc=kc, q=q, ts=ts, tn=tn: e.activation(out=self.hT[:, kc, ts:ts + tn], in_=self.tmpf[q][:, :tn], func=AF.Identity, bias=Bsh(kc)),
                     reads=[self.r_tmpf[q]] + extra_reads, writes=[self.r_h[kc][ti]])

    def final_norm(self):
        P = self.P
        tiles = [1, 2]
        self.rms_stats(tiles)
        outs = []
        for ti in tiles:
            ts, tn = TILES[ti]
            for kc in range(KC):
                q = kc % 3
                P.op("dve", lambda e, kc=kc, q=q, ts=ts, tn=tn: e.scalar_tensor_tensor(out=self.tmpf[q][:, :tn], in0=self.xT[:, kc, ts:ts + tn], scalar=self.V("final_g", kc, 1),
                                                                                     in1=self.rstd[:, ts:ts + tn], op0=ALU.mult, op1=ALU.mult),
                     reads=[self.r_x[kc][ti], self.r_rstd[ti], self.r_vecs], writes=[self.r_tmpf[q]])
                outs.append(P.dma("sp", lambda e, kc=kc, q=q, ts=ts, tn=tn: e.dma_start(out=self.d_out[kc * 128:(kc + 1) * 128, ts - HALO:ts - HALO + tn], in_=self.tmpf[q][:, :tn]),
                                  reads=[self.r_tmpf[q]]))
        return outs

    def out_proj(self, tiles, rhs_of):
        P = self.P
        nb = 0
        for s in range(8):
            w, wres = self.next_w()
            for ti in tiles:
                ts, tn = TILES[ti]
                for b in range(2):
                    m = 2 * s + b
                    pbi = nb % 4
                    nb += 1
                    po = self.pb[pbi]
                    for cc in range(KC):
                        rhs, rres = rhs_of(cc, ti)
                        P.op("pe", lambda e, po=po, w=w, b=b, cc=cc, rhs=rhs, tn=tn: e.matmul(po[:, :tn], w[:, b, cc * 128:(cc + 1) * 128], rhs, start=(cc == 0), stop=(cc == KC - 1)),
                             reads=[wres, rres], writes=[self.r_pb[pbi]])
                    P.op("dve", lambda e, po=po, m=m, ts=ts, tn=tn: e.scalar_tensor_tensor(out=self.xT[:, m, ts:ts + tn], in0=po[:, :tn], scalar=self.modv[:, 32 + m:33 + m],
                                                                                         in1=self.xT[:, m, ts:ts + tn], op0=ALU.mult, op1=ALU.add),
                         reads=[self.r_pb[pbi], self.r_modv, self.r_x[m][ti]], writes=[self.r_x[m][ti]])

    def sg_layer(self, i, j, tiles):
        P = self.P
        chunks = [c for ti in tiles for c in range(TILES[ti][0] // 128, (TILES[ti][0] + TILES[ti][1]) // 128)]
        tile_of_chunk = {c: ti for ti in tiles for c in range(TILES[ti][0] // 128, (TILES[ti][0] + TILES[ti][1]) // 128)}
        bigv = self.big[:].rearrange("p (i c) -> p i c", i=NT // 128)
        scr = self.scr
        wTm = scr[:, 0:1024].bitcast(BF16).rearrange("p (g t) -> p g t", g=16)
        bias2 = scr[:, 1024:3072].rearrange("p (g t) -> p g t", g=16)
        ssum = scr[:, 3072:3072 + 72].rearrange("p (i s) -> p i s", i=9)
        ssq = scr[:, 3200:3200 + 72].rearrange("p (i s) -> p i s", i=9)
        st4 = scr[:, 3328:3328 + 64]
        junk = scr[:, 3456:3456 + 128].bitcast(BF16)
        r_w = Res("sg_wTm"); r_b2 = Res("sg_bias2"); r_st = Res("sg_stats"); r_junk = Res("junk")
        P.dma("pool", lambda e: e.dma_start(out=wTm, in_=self.d_sgw[j].rearrange("p (g t) -> p g t", g=16)), writes=[r_w], reads=[self.r_scr])
        msk = self.cm[:, 2, :]
        mskb = bass.AP(msk.tensor, msk.offset, [msk.ap[0], [0, 16], msk.ap[1]])
        P.op("dve", lambda e: e.tensor_tensor(out=wTm, in0=wTm, in1=mskb, op=ALU.mult), reads=[r_w, self.r_cm], writes=[r_w])
        sgb = self.d_sgb[j]
        P.dma("sp", lambda e: e.dma_start(out=bias2, in_=bass.AP(sgb.tensor, sgb.offset, [[0, 128], [128, 16], [1, 128]])), writes=[r_b2], reads=[self.r_scr])
        P.op("dve", lambda e: e.memset(scr[:, 3072:3456], 0.0), writes=[r_st], reads=[self.r_scr])
        for g4 in range(4):
            pr = self.pb[4 + (g4 % 2)]
            for gg in range(4):
                g = g4 * 4 + gg
                P.op("pe", lambda e, pr=pr, g=g, gg=gg: e.matmul(pr[:, gg * 128:(gg + 1) * 128], self.cm[:, 1, :], wTm[:, g, :], start=True, stop=True),
                     reads=[r_w, self.r_cm], writes=[self.r_pb[4 + (g4 % 2)]])
            for gg in range(4):
                g = g4 * 4 + gg
                P.op("dve", lambda e, pr=pr, g=g, gg=gg: e.scalar_tensor_tensor(out=bias2[:, g, :], in0=pr[:, gg * 128:(gg + 1) * 128], scalar=self.V("sg_ln_b%d" % j, g, 1),
                                                                                 in1=bias2[:, g, :], op0=ALU.mult, op1=ALU.add),
                     reads=[self.r_pb[4 + (g4 % 2)], self.r_vecs, r_b2], writes=[r_b2])
        nb = 0
        for s in range(8):
            w, wres = self.next_w()
            for c in chunks:
                ti = tile_of_chunk[c]
                pbi = nb % 4
                nb += 1
                pv = self.pb[pbi]
                for kc in range(KC):
                    P.op("pe", lambda e, pv=pv, w=w, kc=kc, c=c: e.matmul(pv[:, 0:256].rearrange("p (b f) -> p b f", b=2), self.hT[:, kc, c * 128:(c + 1) * 128],
                                                                          w[:, :, kc * 128:(kc + 1) * 128], start=(kc == 0), stop=(kc == KC - 1)),
                         reads=[wres, self.r_h[kc][ti]], writes=[self.r_pb[pbi]])
                P.op("act", lambda e, pv=pv, c=c, s=s: e.activation(out=bigv[:, c, s * 256:(s + 1) * 256], in_=pv[:, 0:256], func=AF.Gelu_apprx_tanh, accum_out=ssum[:, c, s:s + 1]),
                     reads=[self.r_pb[pbi]], writes=[self.r_big[2 * s][ti], self.r_big[2 * s + 1][ti], r_st])
                P.op("act", lambda e, c=c, s=s: e.activation(out=junk, in_=bigv[:, c, s * 256:(s + 1) * 256], func=AF.Square, accum_out=ssq[:, c, s:s + 1]),
                     reads=[self.r_big[2 * s][ti], self.r_big[2 * s + 1][ti]], writes=[r_junk, r_st])
        for c in chunks:
            ti = tile_of_chunk[c]
            allbig = [self.r_big[k][ti] for k in range(KC)]
            P.op("dve", lambda e, c=c: e.tensor_reduce(out=st4[:, 0:1], in_=ssum[:, c, :], axis=AX.X, op=ALU.add), reads=[r_st], writes=[r_st])
            P.op("dve", lambda e, c=c: e.tensor_reduce(out=st4[:, 1:2], in_=ssq[:, c, :], axis=AX.X, op=ALU.add), reads=[r_st], writes=[r_st])
            P.op("dve", lambda e: e.tensor_scalar(out=st4[:, 2:3], in0=st4[:, 0:1], scalar1=1.0 / D, scalar2=None, op0=ALU.mult), reads=[r_st], writes=[r_st])
            P.op("dve", lambda e: e.tensor_tensor(out=st4[:, 3:4], in0=st4[:, 2:3], in1=st4[:, 2:3], op=ALU.mult), reads=[r_st], writes=[r_st])
            P.op("dve", lambda e: e.scalar_tensor_tensor(out=st4[:, 4:5], in0=st4[:, 1:2], scalar=1.0 / D, in1=st4[:, 3:4], op0=ALU.mult, op1=ALU.subtract),
                 reads=[r_st], writes=[r_st])
            P.op("act", lambda e: e.activation(out=st4[:, 5:6], in_=st4[:, 4:5], func=AF.Sqrt, bias=self.V("eps_ln"), scale=1.0), reads=[r_st, self.r_vecs], writes=[r_st])
            P.op("dve", lambda e: e.reciprocal(out=st4[:, 6:7], in_=st4[:, 5:6]), reads=[r_st], writes=[r_st])
            P.op("dve", lambda e, c=c: e.tensor_scalar(out=bigv[:, c, :], in0=bigv[:, c, :], scalar1=st4[:, 2:3], scalar2=st4[:, 6:7], op0=ALU.subtract, op1=ALU.mult),
                 reads=allbig + [r_st], writes=allbig)
        bigg = self.big[:].rearrange("p (i c t) -> p c i t", i=NT // 128, c=KC)
        b2b = lambda g, n: bass.AP(bias2[:, g, :].tensor, bias2[:, g, :].offset, [bias2[:, g, :].ap[0], [0, n], bias2[:, g, :].ap[1]])
        for c in range(KC):
            w, wres = self.next_w()
            for ti in tiles:
                ts, tn = TILES[ti]
                nch = tn // 128
                c0 = ts // 128
                pu, pz, pf = self.pb[0 + 3 * (ti % 2)], self.pb[1 + 3 * (ti % 2)], self.pb[2 + 3 * (ti % 2)]
                ru, rz, rf = self.r_pb[0 + 3 * (ti % 2)], self.r_pb[1 + 3 * (ti % 2)], self.r_pb[2 + 3 * (ti % 2)]
                for kc in range(KC):
                    P.op("pe", lambda e, pu=pu, w=w, kc=kc, ts=ts, tn=tn: e.matmul(pu[:, :tn], w[:, 0, kc * 128:(kc + 1) * 128], self.hT[:, kc, ts:ts + tn], start=(kc == 0), stop=(kc == KC - 1)),
                         reads=[wres, self.r_h[kc][ti]], writes=[ru])
                for kc in range(KC):
                    P.op("pe", lambda e, pz=pz, w=w, kc=kc, ts=ts, tn=tn: e.matmul(pz[:, :tn], w[:, 1, kc * 128:(kc + 1) * 128], self.hT[:, kc, ts:ts + tn], start=(kc == 0), stop=(kc == KC - 1)),
                         reads=[wres, self.r_h[kc][ti]], writes=[rz])
                for q in range(nch):
                    P.op("pe", lambda e, pf=pf, q=q, c=c, c0=c0: e.matmul(pf[:, q * 128:(q + 1) * 128], bigv[:, c0 + q, c * 128:(c + 1) * 128], wTm[:, c, :], start=True, stop=True),
                         reads=[self.r_big[c][ti], r_w], writes=[rf])
                t0, t1, t2 = self.tmpf
                P.op("act", lambda e, pu=pu, tn=tn: e.activation(out=t0[:, :tn], in_=pu[:, :tn], func=AF.Gelu_apprx_tanh), reads=[ru], writes=[self.r_tmpf[0]])
                P.op("act", lambda e, pz=pz, tn=tn: e.activation(out=t1[:, :tn], in_=pz[:, :tn], func=AF.Silu), reads=[rz], writes=[self.r_tmpf[1]])
                P.op("dve", lambda e, tn=tn: e.tensor_tensor(out=t0[:, :tn], in0=t0[:, :tn], in1=t1[:, :tn], op=ALU.mult), reads=[self.r_tmpf[0], self.r_tmpf[1]], writes=[self.r_tmpf[0]])
                P.op("dve", lambda e, pf=pf, tn=tn, nch=nch, c=c: e.scalar_tensor_tensor(out=t2[:, :tn].rearrange("p (i t) -> p i t", i=nch), in0=pf[:, :tn].rearrange("p (i t) -> p i t", i=nch),
                                                                                         scalar=self.V("sg_ln_g%d" % j, c, 1), in1=b2b(c, nch), op0=ALU.mult, op1=ALU.add),
                     reads=[rf, self.r_vecs, r_b2], writes=[self.r_tmpf[2]])
                P.op("dve", lambda e, tn=tn, nch=nch, c=c, c0=c0: e.tensor_tensor(out=bigg[:, c, c0:c0 + nch, :], in0=t2[:, :tn].rearrange("p (i t) -> p i t", i=nch),
                                                                                 in1=t0[:, :tn].rearrange("p (i t) -> p i t", i=nch), op=ALU.mult),
                     reads=[self.r_tmpf[0], self.r_tmpf[2]], writes=[self.r_big[c][ti]])
        self.out_proj(tiles, lambda cc, ti: (bigg[:, cc, TILES[ti][0] // 128:(TILES[ti][0] + TILES[ti][1]) // 128, :], self.r_big[cc][ti]))
        self.barrier()

    def barrier(self):
        P = self.P
        toks = [P.last_tok.get(e) for e in ("pe", "act", "dve")] + [v for k, v in P.dma_last.items() if k[1] != "pool"]
        P.op("dve", lambda e: e.memset(self.small[:, 63:64], 0.0), extra=toks, writes=[self.r_scr, self.r_small])
        P.op("act", lambda e: e.activation(out=self.small[:, 62:63], in_=self.small[:, 63:64], func=AF.Copy), reads=[self.r_small, self.r_scr], writes=[self.r_small])

    def swa_layer(self, i, j, tiles):
        P = self.P
        scr = self.scr
        NCH = NT // 128
        cosT = scr[:, 0:1152]
        sinS = scr[:, 1152:2304]
        kdup = scr[:, 2304:4608].bitcast(BF16).rearrange("p (g t) -> p g t", g=4)
        vtok = scr[:, 4608:5760].bitcast(BF16).rearrange("p (i c) -> p i c", i=NCH)
        amask = scr[:, 5760:6144].bitcast(BF16).rearrange("p (v k) -> p v k", v=3)
        qrot = scr[:, 6144:6720].bitcast(BF16)
        zs = scr[:, 6720:7296].bitcast(BF16)
        qsw = scr[:, 7296:7808]
        Pm = [scr[:, 7808:7936].bitcast(BF16), scr[:, 7936:8064 - 32].bitcast(BF16)] if False else None
        r_cos = Res("cos"); r_sin = Res("sin"); r_kd = [Res("kd%d" % g) for g in range(4)]; r_vt = [Res("vt%d" % c) for c in range(NCH)]
        r_am = Res("amask"); r_qr = [Res("qr%d" % t) for t in range(3)]; r_zs = [Res("zs%d" % t) for t in range(3)]; r_qsw = Res("qsw")
        Pm = [self.sq[0][:, 0:256], self.sq[1][:, 0:256]]; r_Pm = self.r_sq
        PT = [self.sq[0][:, 256:512].rearrange("p (k q) -> p k q", k=2), self.sq[1][:, 256:512].rearrange("p (k q) -> p k q", k=2)]
        r_PT = [Res("PT0"), Res("PT1")]
        otmp = self.rstd[:, 0:128]; r_ot = Res("otmp")
        P.dma("pool", lambda e: e.dma_start(out=amask, in_=self.d_msk), writes=[r_am], reads=[self.r_scr])
        for ti in tiles:
            ts, tn = TILES[ti]
            pi_ = self.tmpf[2][:, :tn].bitcast(I32)
            pos = self.d_pos
            P.dma("sp", lambda e, pi_=pi_, ts=ts, tn=tn: e.dma_start(out=pi_, in_=bass.AP(pos.tensor, pos.offset + ts, [[0, 128], [1, tn]])), writes=[self.r_tmpf[2]])
            a0 = self.tmpf[0][:, :tn]; a1 = self.tmpf[1][:, :tn]
            P.op("dve", lambda e, a0=a0, pi_=pi_: e.tensor_copy(out=a0, in_=pi_), reads=[self.r_tmpf[2]], writes=[self.r_tmpf[0]])
            C1 = 6.28125
            C2 = float(2 * np.pi - 6.28125)
            TWO_PI = float(2 * np.pi)
            PI = float(np.pi)
            rt = [self.r_tmpf[0], self.r_tmpf[1], self.r_tmpf[2]]
            P.op("dve", lambda e, a0=a0: e.tensor_scalar(out=a0, in0=a0, scalar1=self.V("inv_freq"), scalar2=None, op0=ALU.mult), reads=[rt[0], self.r_vecs], writes=[rt[0]])
            P.op("dve", lambda e, a0=a0, a1=a1: e.tensor_scalar(out=a1, in0=a0, scalar1=float(1.0 / (2 * np.pi)), scalar2=None, op0=ALU.mult), reads=[rt[0]], writes=[rt[1]])
            P.op("dve", lambda e, a1=a1, pi_=pi_: e.tensor_copy(out=pi_, in_=a1), reads=[rt[1]], writes=[rt[2]])
            P.op("dve", lambda e, a1=a1, pi_=pi_: e.tensor_copy(out=a1, in_=pi_), reads=[rt[2]], writes=[rt[1]])
            P.op("dve", lambda e, a0=a0, a1=a1: e.scalar_tensor_tensor(out=a0, in0=a1, scalar=-C1, in1=a0, op0=ALU.mult, op1=ALU.add), reads=[rt[0], rt[1]], writes=[rt[0]])
            P.op("dve", lambda e, a0=a0, a1=a1: e.scalar_tensor_tensor(out=a0, in0=a1, scalar=-C2, in1=a0, op0=ALU.mult, op1=ALU.add), reads=[rt[0], rt[1]], writes=[rt[0]])
            a2 = self.tmpf[2][:, :tn]
            P.op("dve", lambda e, a0=a0, a1=a1: e.tensor_scalar(out=a1, in0=a0, scalar1=PI, scalar2=TWO_PI, op0=ALU.is_gt, op1=ALU.mult), reads=[rt[0]], writes=[rt[1]])
            P.op("dve", lambda e, a0=a0, a1=a1: e.tensor_tensor(out=a0, in0=a0, in1=a1, op=ALU.subtract), reads=[rt[0], rt[1]], writes=[rt[0]])
            P.op("dve", lambda e, a0=a0, a1=a1: e.tensor_scalar(out=a1, in0=a0, scalar1=-PI, scalar2=TWO_PI, op0=ALU.is_lt, op1=ALU.mult), reads=[rt[0]], writes=[rt[1]])
            P.op("dve", lambda e, a0=a0, a1=a1: e.tensor_tensor(out=a0, in0=a0, in1=a1, op=ALU.add), reads=[rt[0], rt[1]], writes=[rt[0]])
            P.op("act", lambda e, a0=a0, ts=ts, tn=tn: e.activation(out=sinS[:, ts:ts + tn], in_=a0, func=AF.Sin, scale=self.V("rope_sign")),
                 reads=[rt[0], self.r_vecs, self.r_scr], writes=[r_sin])
            P.op("dve", lambda e, a0=a0, a2=a2: e.tensor_scalar(out=a2, in0=a0, scalar1=float(np.pi / 2), scalar2=None, op0=ALU.add), reads=[rt[0]], writes=[rt[2]])
            P.op("dve", lambda e, a2=a2, a1=a1: e.tensor_scalar(out=a1, in0=a2, scalar1=PI, scalar2=TWO_PI, op0=ALU.is_gt, op1=ALU.mult), reads=[rt[2]], writes=[rt[1]])
            P.op("dve", lambda e, a2=a2, a1=a1: e.tensor_tensor(out=a2, in0=a2, in1=a1, op=ALU.subtract), reads=[rt[2], rt[1]], writes=[rt[2]])
            P.op("act", lambda e, a2=a2, ts=ts, tn=tn: e.activation(out=cosT[:, ts:ts + tn], in_=a2, func=AF.Sin), reads=[rt[2], self.r_scr], writes=[r_cos])

        def rope(psrc, rsrc, ts, tn, out_ap, out_res, dup=None):
            for (a, b) in ((0, 32), (32, 0), (64, 96), (96, 64)):
                eng = "act" if a in (0, 64) else "dve"
                if eng == "act":
                    P.op("act", lambda e, a=a, b=b: e.activation(out=qsw[a:a + 32, :tn], in_=psrc[b:b + 32, :tn], func=AF.Copy), reads=[rsrc], writes=[r_qsw])
                else:
                    P.op("dve", lambda e, a=a, b=b: e.tensor_copy(out=qsw[a:a + 32, :tn], in_=psrc[b:b + 32, :tn]), reads=[rsrc], writes=[r_qsw])
            t0 = self.tmpf[0][:, :tn]
            P.op("dve", lambda e: e.tensor_tensor(out=t0, in0=psrc[:, :tn], in1=cosT[:, ts:ts + tn], op=ALU.mult), reads=[rsrc, r_cos], writes=[self.r_tmpf[0]])
            P.op("dve", lambda e: e.tensor_tensor(out=qsw[:, :tn], in0=qsw[:, :tn], in1=sinS[:, ts:ts + tn], op=ALU.mult), reads=[r_qsw, r_sin], writes=[r_qsw])
            P.op("dve", lambda e: e.tensor_tensor(out=out_ap, in0=t0, in1=qsw[:, :tn], op=ALU.add), reads=[self.r_tmpf[0], r_qsw], writes=out_res)
            if dup is not None:
                dup()

        w, wres = self.next_w()
        for b in range(2):
            for ti in tiles:
                ts, tn = TILES[ti]
                pk = self.pb[ti % 2]; rk = self.r_pb[ti % 2]
                for kc in range(KC):
                    P.op("pe", lambda e, pk=pk, w=w, b=b, kc=kc, ts=ts, tn=tn: e.matmul(pk[:, :tn], w[:, b, kc * 128:(kc + 1) * 128], self.hT[:, kc, ts:ts + tn], start=(kc == 0), stop=(kc == KC - 1)),
                         reads=[wres, self.r_h[kc][ti]], writes=[rk])
                g0, g1 = 2 * b, 2 * b + 1
                ktmp = self.tmpf[1][:, :tn]

                def dup(ts=ts, tn=tn, g0=g0, g1=g1, ktmp=ktmp):
                    P.op("act", lambda e: e.activation(out=kdup[0:64, g0, ts:ts + tn], in_=ktmp[0:64, :], func=AF.Copy), reads=[self.r_tmpf[1]], writes=[r_kd[g0]])
                    P.op("act", lambda e: e.activation(out=kdup[64:128, g0, ts:ts + tn], in_=ktmp[0:64, :], func=AF.Copy), reads=[self.r_tmpf[1]], writes=[r_kd[g0]])
                    P.op("dve", lambda e: e.tensor_copy(out=kdup[0:64, g1, ts:ts + tn], in_=ktmp[64:128, :]), reads=[self.r_tmpf[1]], writes=[r_kd[g1]])
                    P.op("dve", lambda e: e.tensor_copy(out=kdup[64:128, g1, ts:ts + tn], in_=ktmp[64:128, :]), reads=[self.r_tmpf[1]], writes=[r_kd[g1]])
                rope(pk, rk, ts, tn, ktmp, [self.r_tmpf[1]], dup)
        w, wres = self.next_w()
        chunks = [c for ti in tiles for c in range(TILES[ti][0] // 128, (TILES[ti][0] + TILES[ti][1]) // 128)]
        tile_of_chunk = {c: ti for ti in tiles for c in range(TILES[ti][0] // 128, (TILES[ti][0] + TILES[ti][1]) // 128)}
        for c in chunks:
            ti = tile_of_chunk[c]
            pv = self.pb[2 + c % 2]; rv = self.r_pb[2 + c % 2]
            for kc in range(KC):
                P.op("pe", lambda e, pv=pv, w=w, kc=kc, c=c: e.matmul(pv[:, 0:256].rearrange("p (b f) -> p b f", b=2), self.hT[:, kc, c * 128:(c + 1) * 128],
                                                                      w[:, :, kc * 128:(kc + 1) * 128], start=(kc == 0), stop=(kc == KC - 1)),
                     reads=[wres, self.r_h[kc][ti]], writes=[rv])
            P.op("act", lambda e, pv=pv, c=c: e.activation(out=vtok[:, c, :], in_=pv[:, 0:256], func=AF.Copy), reads=[rv], writes=[r_vt[c]])

        bigg = self.big[:].rearrange("p (i c t) -> p c i t", i=NCH, c=KC)
        ptp = self.pb[6][:, 0:128].bitcast(BF16).rearrange("p (k q) -> p k q", k=2)
        unit = 0
        for c in range(KC):
            g = c // 4
            w, wres = self.next_w()
            for ti in tiles:
                ts, tn = TILES[ti]
                pq = self.pb[ti % 2]; rq = self.r_pb[ti % 2]
                pz = self.pb[2 + ti % 2]; rz = self.r_pb[2 + ti % 2]
                for kc in range(KC):
                    P.op("pe", lambda e, pq=pq, w=w, kc=kc, ts=ts, tn=tn: e.matmul(pq[:, :tn], w[:, 0, kc * 128:(kc + 1) * 128], self.hT[:, kc, ts:ts + tn], start=(kc == 0), stop=(kc == KC - 1)),
                         reads=[wres, self.r_h[kc][ti]], writes=[rq])
                for kc in range(KC):
                    P.op("pe", lambda e, pz=pz, w=w, kc=kc, ts=ts, tn=tn: e.matmul(pz[:, :tn], w[:, 1, kc * 128:(kc + 1) * 128], self.hT[:, kc, ts:ts + tn], start=(kc == 0), stop=(kc == KC - 1)),
                         reads=[wres, self.r_h[kc][ti]], writes=[rz])
                P.op("act", lambda e, pz=pz, ts=ts, tn=tn: e.activation(out=zs[:, ts:ts + tn], in_=pz[:, :tn], func=AF.Silu), reads=[rz], writes=[r_zs[ti]])
                rope(pq, rq, ts, tn, qrot[:, ts:ts + tn], [r_qr[ti]])
                for ci in range(ts // 128, (ts + tn) // 128):
                    halo = (ci == 0)
                    nkb = 1 if halo else 2
                    nk = 128 * nkb
                    k0 = 0 if halo else (ci - 1) * 128
                    var = 0 if ci > 1 else 2
                    mk = amask[:, 0, 128:256] if halo else amask[:, var, :]
                    po = self.pb[7]; rpo = self.r_pb[7]
                    for hh in range(2):
                        pp = hh * 64
                        hidx = 2 * c + hh
                        u = unit % 2
                        unit += 1
                        psc = self.pb[4 + u]; rps = self.r_pb[4 + u]
                        st = self.small[:, (unit % 6) * 8:(unit % 6) * 8 + 8]; r_st = self.r_small
                        sink = self.V("sinks", hidx, 1)
                        P.op("pe", lambda e, psc=psc, pp=pp, ci=ci, k0=k0, nk=nk, g=g: e.matmul(psc[:, :nk], qrot[pp:pp + 64, ci * 128:(ci + 1) * 128], kdup[pp:pp + 64, g, k0:k0 + nk], start=True, stop=False),
                             reads=[r_qr[ti], r_kd[g]], writes=[rps])
                        P.op("pe", lambda e, psc=psc, nk=nk, mk=mk: e.matmul(psc[:, :nk], self.cm[:, 0, :], mk, start=False, stop=True), reads=[self.r_cm, r_am], writes=[rps])
                        P.op("dve", lambda e, psc=psc, nk=nk, st=st: e.reduce_max(out=st[:, 0:1], in_=psc[:, :nk], axis=AX.X), reads=[rps], writes=[r_st])
                        P.op("dve", lambda e, st=st, sink=sink: e.tensor_scalar(out=st[:, 1:2], in0=st[:, 0:1], scalar1=0.125, scalar2=sink, op0=ALU.mult, op1=ALU.max), reads=[r_st, self.r_vecs], writes=[r_st])
                        P.op("dve", lambda e, st=st: e.tensor_scalar(out=st[:, 2:3], in0=st[:, 1:2], scalar1=-1.0, scalar2=None, op0=ALU.mult), reads=[r_st], writes=[r_st])
                        P.op("act", lambda e, psc=psc, nk=nk, st=st, u=u: e.activation(out=Pm[u][:, :nk], in_=psc[:, :nk], func=AF.Exp, bias=st[:, 2:3], scale=0.125, accum_out=st[:, 3:4]),
                             reads=[rps, r_st], writes=[r_Pm[u], r_st])
                        P.op("act", lambda e, st=st, sink=sink: e.activation(out=st[:, 4:5], in_=sink, func=AF.Exp, bias=st[:, 2:3], scale=1.0), reads=[r_st, self.r_vecs], writes=[r_st])
                        P.op("dve", lambda e, st=st: e.tensor_tensor(out=st[:, 5:6], in0=st[:, 3:4], in1=st[:, 4:5], op=ALU.add), reads=[r_st], writes=[r_st])
                        P.op("dve", lambda e, st=st: e.reciprocal(out=st[:, 6:7], in_=st[:, 5:6]), reads=[r_st], writes=[r_st])
                        P.op("dve", lambda e, st=st, u=u, nk=nk: e.tensor_scalar(out=Pm[u][:, :nk], in0=Pm[u][:, :nk], scalar1=st[:, 6:7], scalar2=None, op0=ALU.mult), reads=[r_st, r_Pm[u]], writes=[r_Pm[u]])
                        for kb in range(nkb):
                            P.op("pe", lambda e, u=u, kb=kb: e.transpose(ptp[:, kb, :], Pm[u][:, kb * 128:(kb + 1) * 128], self.cm[:, 0, :]), reads=[r_Pm[u], self.r_cm], writes=[self.r_pb[6]])
                        P.op("act", lambda e, u=u, nkb=nkb: e.activation(out=PT[u][:, 0:nkb, :], in_=ptp[:, 0:nkb, :], func=AF.Copy), reads=[self.r_pb[6]], writes=[r_PT[u]])
                        for kb in range(nkb):
                            kc_ = ci if halo else ci - 1 + kb
                            P.op("pe", lambda e, u=u, kb=kb, kc_=kc_, hh=hh, g=g, nkb=nkb: e.matmul(po[0:64, hh * 128:(hh + 1) * 128], vtok[:, kc_, g * 64:(g + 1) * 64], PT[u][:, kb, :],
                                                                                                   start=(kb == 0), stop=(kb == nkb - 1)),
                                 reads=[r_PT[u], r_vt[kc_]], writes=[rpo])
                        if hh == 0:
                            P.op("dve", lambda e, c=c, ci=ci: e.tensor_tensor(out=bigg[0:64, c, ci, :], in0=po[0:64, 0:128], in1=zs[0:64, ci * 128:(ci + 1) * 128], op=ALU.mult),
                                 reads=[rpo, r_zs[ti]], writes=[self.r_big[c][ti]])
                        else:
                            P.op("act", lambda e: e.activation(out=otmp[64:128, :], in_=po[0:64, 128:256], func=AF.Copy), reads=[rpo], writes=[r_ot])
                            P.op("dve", lambda e, c=c, ci=ci: e.tensor_tensor(out=bigg[64:128, c, ci, :], in0=otmp[64:128, :], in1=zs[64:128, ci * 128:(ci + 1) * 128], op=ALU.mult),
                                 reads=[r_ot, r_zs[ti]], writes=[self.r_big[c][ti]])
        self.out_proj(tiles, lambda cc, ti: (bigg[:, cc, TILES[ti][0] // 128:(TILES[ti][0] + TILES[ti][1]) // 128, :], self.r_big[cc][ti]))
        self.barrier()

    def rwkv_layer(self, i, j, tiles):
        P = self.P
        scr = self.scr
        NCH = NT // 128
        self.barrier()
        arenas = [[scr, 0, 7824], [self.rstd, 0, NT], [self.big[:, 0:2048].bitcast(F32), 0, 1024], [self.tmpf[2], 0, 512],
                  [self.sq[0][:].bitcast(F32), 0, 256], [self.sq[1][:].bitcast(F32), 0, 256],
                  [self.tmpf[0][:, 128:512], 0, 384], [self.tmpf[1][:, 128:512], 0, 384]]

        def alloc(n, dt=F32):
            for ar in arenas:
                if ar[1] + n <= ar[2]:
                    a = ar[0][:, ar[1]:ar[1] + n]
                    ar[1] += n
                    return a if dt == F32 else a.bitcast(BF16)
            raise AssertionError("rwkv scratch exhausted")
        TN = 512
        pf = alloc(516)
        F_L = alloc(TN); F_A = alloc(TN); F_K = alloc(TN); F_T = alloc(TN); F_R = alloc(TN); F_B = alloc(TN)
        E_pos = alloc(256, BF16); E_neg = alloc(256, BF16); E_end = alloc(256, BF16); E_prev = alloc(256, BF16)
        O_rt = alloc(256, BF16); O_kk = alloc(256, BF16); O_kh = alloc(256, BF16); O_bh = alloc(256, BF16)
        O_kb = alloc(256, BF16); O_bb = alloc(256, BF16); O_zs = alloc(256, BF16); O_v = alloc(256, BF16); O_t = alloc(256, BF16)
        Vaug = alloc(512, BF16).rearrange("p (c h n) -> p c h n", c=4, h=2)
        Kbt = alloc(256, BF16).rearrange("p (c n) -> p c n", c=4)
        Bbt = alloc(256, BF16).rearrange("p (c n) -> p c n", c=4)
        dwl = alloc(512, BF16); dal = alloc(512, BF16)
        wlo = alloc(64, BF16); alo = alloc(64, BF16)
        identf = alloc(128); mtb = alloc(192, BF16).rearrange("p (i n) -> p i n", i=3); nbuf = alloc(192).rearrange("p (i n) -> p i n", i=3)
        r_idf = Res("identf"); r_mtb = Res("mtb"); r_nb = Res("nbuf")
        Hf = alloc(128); Hb = alloc(64, BF16)
        gC = alloc(8); carry = alloc(8); stt = alloc(16)
        Ytok = alloc(128)
        mats = {}
        for hp in range(2):
            for nm in ("AkT", "ArkT", "ArbT", "Xa0", "Xb0", "Xa1", "Xb1", "G0", "G1", "W", "Un"):
                mats[(nm, hp)] = (alloc(64, BF16), Res(nm + str(hp)))
        yn = alloc(64, BF16); r_yn = Res("yn")
        R = lambda n: Res(n)
        r_pf = R("pf"); r_FL = R("FL"); r_FA = R("FA"); r_FK = R("FK"); r_FT = R("FT"); r_FR = R("FR"); r_FB = R("FB")
        r_E = R("E"); r_O = {n: R(n) for n in ("rt", "kk", "kh", "bh", "kb", "bb", "zs", "v", "t")}
        r_Va = R("Vaug"); r_Kbt = R("Kbt"); r_Bbt = R("Bbt"); r_dwl = R("dwl"); r_dal = R("dal"); r_lo = R("lora")
        r_H = [R("H0"), R("H1")]; r_gC = R("gC"); r_carry = R("carry"); r_stt = R("stt"); r_Y = R("Ytok")
        r_q5 = [R("q5_%d" % q) for q in range(4)]; r_q6 = [R("q6_%d" % q) for q in range(4)]
        cnt = {"q5": 0, "q6": 0}
        pb = self.pb
        ident = self.cm[:, 0, :]; ones_blk = self.cm[:, 3, :]; m_u = self.cm[:, 2, :]; m_su = self.cm[:, 4, :]; m_sl = self.cm[:, 5, :]
        OWN0 = HALO
        P.dma("sp", lambda e: e.dma_start(out=identf, in_=self.d_cm[:, 0, :]), writes=[r_idf], reads=[self.r_scr])
        P.op("dve", lambda e: e.memset(Vaug, 0.0), writes=[r_Va], reads=[self.r_scr])

        def stage(pi, w, wres, b, ti, vec_mu, extra_flagcol=None):
            ts, tn = TILES[ti]
            pp_, rp = pb[pi], self.r_pb[pi]
            for kc in range(KC):
                P.op("pe", lambda e, kc=kc: e.matmul(pp_[:, :tn], w[:, b, kc * 128:(kc + 1) * 128], self.hT[:, kc, ts:ts + tn], start=(kc == 0), stop=(kc == KC - 1)),
                     reads=[wres, self.r_h[kc][ti]], writes=[rp])
            return pp_, rp

        def prev_col(pi, w, wres, b, slot):
            pp_, rp = pb[4], self.r_pb[4]
            for kc in range(KC):
                P.op("pe", lambda e, kc=kc: e.matmul(pp_[:, slot:slot + 1], w[:, b, kc * 128:(kc + 1) * 128], self.hT[:, kc, 127:128], start=(kc == 0), stop=(kc == KC - 1)),
                     reads=[wres, self.r_h[kc][0]], writes=[rp])
            P.op("dve", lambda e: e.tensor_tensor(out=carry[:, slot:slot + 1], in0=pp_[:, slot:slot + 1], in1=self.V("hasprev"), op=ALU.mult),
                 reads=[rp, self.r_vecs], writes=[r_carry])

        def lerp(pp_, rp, tn, slot, mu, out, r_out, act=None):
            P.op("act", lambda e: e.activation(out=pf[:, 1:1 + tn], in_=pp_[:, :tn], func=AF.Copy), reads=[rp], writes=[r_pf])
            P.op("dve", lambda e: e.tensor_copy(out=pf[:, 0:1], in_=carry[:, slot:slot + 1]), reads=[r_carry, r_pf], writes=[r_pf])
            P.op("dve", lambda e: e.tensor_copy(out=carry[:, slot:slot + 1], in_=pf[:, tn:tn + 1]), reads=[r_pf], writes=[r_carry])
            P.op("dve", lambda e: e.tensor_tensor(out=out, in0=pf[:, 0:tn], in1=pf[:, 1:1 + tn], op=ALU.subtract), reads=[r_pf], writes=[r_out])
            P.op("dve", lambda e: e.scalar_tensor_tensor(out=out, in0=out, scalar=mu, in1=pf[:, 1:1 + tn], op0=ALU.mult, op1=ALU.add), reads=[r_pf, r_out, self.r_vecs], writes=[r_out])

        w, wres = self.next_w()
        for b, (dst, rdst, mu) in enumerate(((dwl, r_dwl, self.V("mu_dw")), (dal, r_dal, self.V("mu_da")))):
            prev_col(4, w, wres, b, 4 + b)
            for ti in tiles:
                ts, tn = TILES[ti]
                pp_, rp = stage(b, w, wres, b, ti, None)
                lerp(pp_, rp, tn, 4 + b, mu, F_T, r_FT)
                if b == 0:
                    P.op("act", lambda e, ts=ts, tn=tn, dst=dst: e.activation(out=dst[:, ts - OWN0:ts - OWN0 + tn], in_=F_T, func=AF.Tanh), reads=[r_FT], writes=[rdst])
                else:
                    P.op("act", lambda e, ts=ts, tn=tn, dst=dst: e.activation(out=dst[:, ts - OWN0:ts - OWN0 + tn], in_=F_T, func=AF.Copy), reads=[r_FT], writes=[rdst])

        bigg = self.big[:].rearrange("p (i c t) -> p c i t", i=NCH, c=KC)

        def q5():
            b_ = (5, 0, 1)[cnt["q5"] % 3]
            cnt["q5"] += 1
            return pb[b_][:, 0:128], self.r_pb[b_]

        def q6():
            b_ = (6, 2, 3)[cnt["q6"] % 3]
            cnt["q6"] += 1
            return pb[b_][:, 0:128], self.r_pb[b_]

        STAGE = int(os.environ.get("RW_STAGE", "9"))
        def do_block(c):
            w1, w1res = self.next_w()
            w2, w2res = self.next_w()
            prev_col(4, w1, w1res, 0, 0); prev_col(4, w1, w1res, 1, 1); prev_col(4, w2, w2res, 0, 2); prev_col(4, w2, w2res, 1, 3)
            P.dma("pool", lambda e, c=c: e.dma_start(out=wlo[0:96, :], in_=self.d_wlora[0, :, c * 128:(c + 1) * 128]), writes=[r_lo], reads=[self.r_scr])
            P.dma("pool", lambda e, c=c: e.dma_start(out=alo[0:96, :], in_=self.d_wlora[1, :, c * 128:(c + 1) * 128]), writes=[r_lo], reads=[self.r_scr])
            P.dma("pool", lambda e, c=c: e.dma_start(out=mtb, in_=self.d_pmt[:, c].rearrange("i p n -> p i n")), writes=[r_mtb], reads=[self.r_scr])
            P.dma("sp", lambda e, c=c: e.dma_start(out=nbuf, in_=self.d_pn[:, c].rearrange("i p n -> p i n")), writes=[r_nb], reads=[self.r_scr])
            P.op("dve", lambda e: e.memset(Hf, 0.0), writes=r_H, reads=[self.r_scr])
            P.op("dve", lambda e: e.memset(Hb, 0.0), writes=r_H, reads=r_H)
            for ip in range(3):
                psf, rpsf = q6()
                P.op("pe", lambda e, psf=psf, ip=ip: e.matmul(psf[:, 0:64], mtb[:, ip, :], Hb[:, 0:64], start=True, stop=True), reads=[r_mtb] + r_H, writes=[rpsf])
                P.op("dve", lambda e, psf=psf, ip=ip: e.tensor_tensor(out=Hf[:, 0:64], in0=psf[:, 0:64], in1=nbuf[:, ip, :], op=ALU.add), reads=[rpsf, r_nb] + r_H, writes=r_H)
                P.op("act", lambda e: e.activation(out=Hb[:, 0:64], in_=Hf[:, 0:64], func=AF.Copy), reads=r_H, writes=r_H)
            P.op("dve", lambda e: e.tensor_copy(out=Hf[0:64, 64:128], in_=identf[0:64, 0:64]), reads=[r_idf] + r_H, writes=r_H)
            P.op("dve", lambda e: e.tensor_copy(out=Hf[64:128, 64:128], in_=identf[64:128, 64:128]), reads=[r_idf] + r_H, writes=r_H)
            P.op("act", lambda e: e.activation(out=Hb, in_=Hf, func=AF.Copy), reads=r_H, writes=r_H)
            def do_tile(ti):
                ts, tn = TILES[ti]
                t0 = ts - OWN0
                cc = slice(0, 128)
                pl, rpl = pb[4], self.r_pb[4]
                P.op("pe", lambda e, t0=t0, tn=tn: e.matmul(pl[:, :tn], wlo[0:96, cc], dwl[0:96, t0:t0 + tn], start=True, stop=True), reads=[r_lo, r_dwl], writes=[rpl])
                P.op("act", lambda e: e.activation(out=F_L, in_=pl[:, :tn], func=AF.Sigmoid, bias=self.V("w0", c, 1)), reads=[rpl, self.r_vecs], writes=[r_FL])
                P.op("pe", lambda e, t0=t0, tn=tn: e.matmul(pl[:, :tn], alo[0:96, cc], dal[0:96, t0:t0 + tn], start=True, stop=True), reads=[r_lo, r_dal], writes=[rpl])
                P.op("act", lambda e: e.activation(out=F_A, in_=pl[:, :tn], func=AF.Sigmoid, bias=self.V("a0", c, 1)), reads=[rpl, self.r_vecs], writes=[r_FA])
                P.op("dve", lambda e: e.tensor_scalar(out=F_L, in0=F_L, scalar1=-DECAY_SCALE, scalar2=None, op0=ALU.mult), reads=[r_FL], writes=[r_FL])
                for q in range(4):
                    sl = slice(q * 128, (q + 1) * 128)
                    P.op("dve", lambda e, sl=sl: e.tensor_tensor_scan(out=F_L[:, sl], data0=F_L[:, sl], data1=F_L[:, sl], initial=0.0, op0=ALU.add, op1=ALU.bypass), reads=[r_FL], writes=[r_FL])
                P.op("act", lambda e: e.activation(out=E_pos, in_=F_L, func=AF.Exp), reads=[r_FL], writes=[r_E])
                P.op("act", lambda e: e.activation(out=E_neg, in_=F_L, func=AF.Exp, scale=-1.0), reads=[r_FL], writes=[r_E])
                for q in range(4):
                    sl = slice(q * 128, (q + 1) * 128)
                    P.op("act", lambda e, sl=sl, q=q: e.activation(out=E_end[:, sl], in_=F_L[:, sl], func=AF.Exp, scale=-1.0, bias=F_L[:, q * 128 + 127:q * 128 + 128]), reads=[r_FL], writes=[r_E])
                    P.op("act", lambda e, q=q: e.activation(out=gC[:, q:q + 1], in_=F_L[:, q * 128 + 127:q * 128 + 128], func=AF.Exp), reads=[r_FL], writes=[r_gC])
                    P.op("dve", lambda e, sl=sl, q=q: e.tensor_copy(out=E_prev[:, q * 128 + 1:q * 128 + 128], in_=E_pos[:, q * 128:q * 128 + 127]), reads=[r_E], writes=[r_E])
                    P.op("dve", lambda e, q=q: e.memset(E_prev[:, q * 128:q * 128 + 1], 1.0), reads=[r_E], writes=[r_E])
                pp_, rp = stage(1, w1, w1res, 1, ti, None)
                lerp(pp_, rp, tn, 1, self.V("mu_k", c, 1), F_K, r_FK)
                P.op("dve", lambda e: e.tensor_scalar(out=F_T, in0=F_K, scalar1=self.V("k_k", c, 1), scalar2=None, op0=ALU.mult), reads=[r_FK, self.r_vecs], writes=[r_FT])
                P.op("dve", lambda e: e.tensor_tensor(out=O_t, in0=F_T, in1=F_T, op=ALU.mult), reads=[r_FT], writes=[r_O["t"]])
                P.op("pe", lambda e: e.matmul(pl[:, :tn], ones_blk, O_t, start=True, stop=True), reads=[self.r_cm, r_O["t"]], writes=[rpl])
                P.op("act", lambda e: e.activation(out=F_B, in_=pl[:, :tn], func=AF.Sqrt), reads=[rpl], writes=[r_FB])
                P.op("dve", lambda e: e.tensor_scalar(out=F_B, in0=F_B, scalar1=1e-12, scalar2=None, op0=ALU.max), reads=[r_FB], writes=[r_FB])
                P.op("dve", lambda e: e.reciprocal(out=F_B, in_=F_B), reads=[r_FB], writes=[r_FB])
                P.op("dve", lambda e: e.tensor_tensor(out=F_T, in0=F_T, in1=F_B, op=ALU.mult), reads=[r_FT, r_FB], writes=[r_FT])
                P.op("dve", lambda e: e.tensor_tensor(out=O_kk, in0=F_T, in1=E_prev, op=ALU.mult), reads=[r_FT, r_E], writes=[r_O["kk"]])
                P.op("dve", lambda e: e.tensor_tensor(out=F_T, in0=F_T, in1=F_A, op=ALU.mult), reads=[r_FT, r_FA], writes=[r_FT])
                P.op("dve", lambda e: e.tensor_tensor(out=O_bh, in0=F_T, in1=E_neg, op=ALU.mult), reads=[r_FT, r_E], writes=[r_O["bh"]])
                P.op("dve", lambda e: e.tensor_tensor(out=O_bb, in0=F_T, in1=E_end, op=ALU.mult), reads=[r_FT, r_E], writes=[r_O["bb"]])
                P.op("dve", lambda e: e.tensor_scalar(out=F_B, in0=F_A, scalar1=self.V("k_a", c, 1), scalar2=self.V("k_a", c, 1), op0=ALU.mult, op1=ALU.subtract), reads=[r_FA, self.r_vecs], writes=[r_FB])
                P.op("dve", lambda e: e.scalar_tensor_tensor(out=F_K, in0=F_B, scalar=1.0, in1=F_K, op0=ALU.add, op1=ALU.mult), reads=[r_FB, r_FK], writes=[r_FK])
                P.op("dve", lambda e: e.tensor_tensor(out=O_kh, in0=F_K, in1=E_neg, op=ALU.mult), reads=[r_FK, r_E], writes=[r_O["kh"]])
                P.op("dve", lambda e: e.tensor_tensor(out=O_kb, in0=F_K, in1=E_end, op=ALU.mult), reads=[r_FK, r_E], writes=[r_O["kb"]])
                pp_, rp = stage(0, w1, w1res, 0, ti, None)
                lerp(pp_, rp, tn, 0, self.V("mu_r", c, 1), F_R, r_FR)
                P.op("dve", lambda e: e.scalar_tensor_tensor(out=O_t, in0=F_R, scalar=self.V("r_k", c, 1), in1=F_K, op0=ALU.mult, op1=ALU.mult), reads=[r_FR, r_FK, self.r_vecs], writes=[r_O["t"]])
                P.op("pe", lambda e: e.matmul(pl[:, :tn], ones_blk, O_t, start=True, stop=True), reads=[self.r_cm, r_O["t"]], writes=[rpl])
                P.op("dve", lambda e: e.tensor_tensor(out=O_rt, in0=F_R, in1=E_pos, op=ALU.mult), reads=[r_FR, r_E], writes=[r_O["rt"]])
                pp_, rp = stage(2, w2, w2res, 0, ti, None)
                lerp(pp_, rp, tn, 2, self.V("mu_v", c, 1), F_R, r_FR)
                P.op("dve", lambda e: e.tensor_tensor(out=F_B, in0=pl[:, :tn], in1=F_R, op=ALU.mult), reads=[rpl, r_FR], writes=[r_FB])
                P.op("act", lambda e: e.activation(out=O_v, in_=F_R, func=AF.Copy), reads=[r_FR], writes=[r_O["v"]])
                pp_, rp = stage(3, w2, w2res, 1, ti, None)
                lerp(pp_, rp, tn, 3, self.V("mu_z", c, 1), F_R, r_FR)
                P.op("act", lambda e: e.activation(out=O_zs, in_=F_R, func=AF.Silu), reads=[r_FR], writes=[r_O["zs"]])
                if c == 0 and ti == 1:
                    for nm_, ap_, rs_ in (("FL", F_L, [r_FL]), ("FA", F_A, [r_FA]), ("k2", F_K, [r_FK]), ("b", F_T, [r_FT]), ("bonus", F_B, [r_FB]), ("Epos", E_pos, [r_E]), ("Eneg", E_neg, [r_E]),
                                          ("Eend", E_end, [r_E]), ("Eprev", E_prev, [r_E]), ("Ort", O_rt, [r_O["rt"]]), ("Okk", O_kk, [r_O["kk"]]), ("Okh", O_kh, [r_O["kh"]]),
                                          ("Obh", O_bh, [r_O["bh"]]), ("Okb", O_kb, [r_O["kb"]]), ("Obb", O_bb, [r_O["bb"]]), ("Ozs", O_zs, [r_O["zs"]]), ("Ov", O_v, [r_O["v"]]), ("gC", gC, [r_gC])):
                        self.dump(nm_, ap_, rs_)
                ptp = pb[7][:, 0:256].bitcast(BF16).rearrange("p (k q) -> p k q", k=4)
                for (src, rs, dst, rd, isv) in ((O_v, r_O["v"], None, r_Va, True), (O_kb, r_O["kb"], Kbt, r_Kbt, False), (O_bb, r_O["bb"], Bbt, r_Bbt, False)):
                    for q in range(4):
                        P.op("pe", lambda e, q=q, src=src: e.transpose(ptp[:, q, :], src[:, q * 128:(q + 1) * 128], ident), reads=[rs, self.r_cm], writes=[self.r_pb[7]])
                    if isv:
                        P.op("act", lambda e: e.activation(out=Vaug[:, :, :, 0:64], in_=ptp.rearrange("p c (h n) -> p c h n", h=2), func=AF.Copy), reads=[self.r_pb[7]], writes=[rd])
                    else:
                        P.op("act", lambda e, dst=dst: e.activation(out=dst, in_=ptp, func=AF.Copy), reads=[self.r_pb[7]], writes=[rd])
                def do_chunk(q):
                    sl = slice(q * 128, (q + 1) * 128)
                    gchunk = (ts // 128) + q
                    def do_head(hp):
                        p0 = hp * 64
                        M = lambda nm: mats[(nm, hp)]
                        rt = O_rt[p0:p0 + 64, sl]; kkt = O_kk[p0:p0 + 64, sl]; kh = O_kh[p0:p0 + 64, sl]; bh = O_bh[p0:p0 + 64, sl]
                        rr = [r_O["rt"], r_O["kk"], r_O["kh"], r_O["bh"]]

                        def mm_mask(lhs, rhs, mask, dst):
                            ps_, rps_ = q5()
                            P.op("pe", lambda e: e.matmul(ps_, lhs, rhs, start=True, stop=True), reads=rr, writes=[rps_])
                            P.op("dve", lambda e: e.tensor_tensor(out=dst[0], in0=ps_, in1=mask, op=ALU.mult), reads=[rps_, self.r_cm], writes=[dst[1]])
                        mm_mask(kh, kkt, m_su, M("AkT"))
                        mm_mask(kh, rt, m_u, M("ArkT"))
                        mm_mask(bh, rt, m_u, M("ArbT"))
                        for (lhs, rhs, mask, nm) in ((kkt, bh, m_sl, "Xa0"), (bh, kkt, m_su, "Xb0")):
                            ps_, rps_ = q5()
                            P.op("pe", lambda e, ps_=ps_, lhs=lhs, rhs=rhs: e.matmul(ps_, lhs, rhs, start=True, stop=True), reads=rr, writes=[rps_])
                            P.op("dve", lambda e, ps_=ps_, mask=mask, nm=nm: e.scalar_tensor_tensor(out=M(nm)[0], in0=ps_, scalar=-1.0, in1=mask, op0=ALU.mult, op1=ALU.mult),
                                 reads=[rps_, self.r_cm], writes=[M(nm)[1]])
                        P.op("dve", lambda e: e.tensor_tensor(out=M("G0")[0], in0=M("Xb0")[0], in1=ident, op=ALU.add), reads=[M("Xb0")[1], self.r_cm], writes=[M("G0")[1]])
                        cur = 0
                        for lvl in range(1, 7):
                            nxt = 1 - cur
                            Xa, Xb = M("Xa%d" % cur), M("Xb%d" % cur)
                            Xan, Xbn = M("Xa%d" % nxt), M("Xb%d" % nxt)
                            Gc, Gn = M("G%d" % cur), M("G%d" % nxt)
                            ps_, rps_ = q5()
                            P.op("pe", lambda e, ps_=ps_, Xa=Xa, Xb=Xb: e.matmul(ps_, Xb[0], Xa[0], start=True, stop=True), reads=[Xa[1], Xb[1]], writes=[rps_])
                            if lvl < 6:
                                ps2, rps2 = q5()
                                P.op("pe", lambda e, ps2=ps2, Xa=Xa, Xb=Xb: e.matmul(ps2, Xa[0], Xb[0], start=True, stop=True), reads=[Xa[1], Xb[1]], writes=[rps2])
                            P.op("act", lambda e, ps_=ps_, Xan=Xan: e.activation(out=Xan[0], in_=ps_, func=AF.Copy), reads=[rps_], writes=[Xan[1]])
                            if lvl < 6:
                                P.op("act", lambda e, ps2=ps2, Xbn=Xbn: e.activation(out=Xbn[0], in_=ps2, func=AF.Copy), reads=[rps2], writes=[Xbn[1]])
                            ps3, rps3 = q5()
                            P.op("pe", lambda e, ps3=ps3, Xan=Xan, Gc=Gc: e.matmul(ps3, Xan[0], Gc[0], start=True, stop=True), reads=[Xan[1], Gc[1]], writes=[rps3])
                            P.op("dve", lambda e, ps3=ps3, Gc=Gc, Gn=Gn: e.tensor_tensor(out=Gn[0], in0=ps3, in1=Gc[0], op=ALU.add), reads=[rps3, Gc[1]], writes=[Gn[1]])
                            cur = nxt
                        G = M("G%d" % cur)
                        if c == 0 and ti == 1 and q == 0 and hp == 0:
                            for nm_ in ("AkT", "ArkT", "ArbT"):
                                self.dump(nm_, M(nm_)[0], [M(nm_)[1]])
                            self.dump("G", G[0], [G[1]])
                        Hh = Hb[p0:p0 + 64, :]
                        Va = Vaug[:, q, hp, :]
                        ps_, rps_ = q6()
                        P.op("pe", lambda e, ps_=ps_: e.matmul(ps_, kkt, Hh, start=True, stop=False), reads=rr + [r_H[hp]], writes=[rps_])
                        P.op("pe", lambda e, ps_=ps_: e.matmul(ps_, M("AkT")[0], Va, start=False, stop=True), reads=[M("AkT")[1], r_Va], writes=[rps_])
                        P.op("act", lambda e, ps_=ps_: e.activation(out=M("W")[0], in_=ps_, func=AF.Copy), reads=[rps_], writes=[M("W")[1]])
                        ps2, rps2 = q6()
                        P.op("pe", lambda e, ps2=ps2, G=G: e.matmul(ps2, G[0], M("W")[0], start=True, stop=True), reads=[G[1], M("W")[1]], writes=[rps2])
                        P.op("act", lambda e, ps2=ps2: e.activation(out=M("Un")[0], in_=ps2, func=AF.Copy, scale=-1.0), reads=[rps2], writes=[M("Un")[1]])
                        ps3, rps3 = q6()
                        P.op("pe", lambda e, ps3=ps3: e.matmul(ps3[:, 0:64], rt, Hh[:, 0:64], start=True, stop=False), reads=rr + [r_H[hp]], writes=[rps3])
                        P.op("pe", lambda e, ps3=ps3: e.matmul(ps3[:, 0:64], M("ArkT")[0], Va[:, 0:64], start=False, stop=False), reads=[M("ArkT")[1], r_Va], writes=[rps3])
                        P.op("pe", lambda e, ps3=ps3: e.matmul(ps3[:, 0:64], M("ArbT")[0], M("Un")[0][:, 0:64], start=False, stop=True), reads=[M("ArbT")[1], M("Un")[1]], writes=[rps3])
                        P.op("act", lambda e, ps3=ps3, p0=p0: e.activation(out=Ytok[:, p0:p0 + 64], in_=ps3[:, 0:64], func=AF.Copy), reads=[rps3], writes=[r_Y])
                        if c == 0 and ti == 1 and q == 0 and hp == 0:
                            self.dump("W", M("W")[0], [M("W")[1]]); self.dump("Un", M("Un")[0], [M("Un")[1]]); self.dump("Ytok0", Ytok, [r_Y])
                        ps4, rps4 = q6()
                        ncol = 64 if hp == 0 else 128
                        P.op("pe", lambda e, ps4=ps4, ncol=ncol: e.matmul(ps4[0:ncol, :], Kbt[:, q, 0:ncol], Va, start=True, stop=False), reads=[r_Kbt, r_Va], writes=[rps4])
                        P.op("pe", lambda e, ps4=ps4, ncol=ncol: e.matmul(ps4[0:ncol, :], Bbt[:, q, 0:ncol], M("Un")[0], start=False, stop=True), reads=[r_Bbt, M("Un")[1]], writes=[rps4])
                        P.op("dve", lambda e, ps4=ps4, p0=p0, q=q: e.scalar_tensor_tensor(out=Hf[p0:p0 + 64, :], in0=Hf[p0:p0 + 64, :], scalar=gC[p0:p0 + 64, q:q + 1], in1=ps4[p0:p0 + 64, :],
                                                                                       op0=ALU.mult, op1=ALU.add), reads=[rps4, r_gC, r_H[hp]], writes=[r_H[hp]])
                        P.op("act", lambda e, p0=p0: e.activation(out=Hb[p0:p0 + 64, :], in_=Hf[p0:p0 + 64, :], func=AF.Copy), reads=[r_H[hp]], writes=[r_H[hp]])
                    for hp_ in range(2):
                        do_head(hp_)
                    if STAGE < 3:
                        return
                    if c == 0 and ti == 1 and q == 0:
                        self.dump("Ytok", Ytok, [r_Y]); self.dump("Hf1", Hf, r_H)
                    Y3 = Ytok.rearrange("p (h n) -> p h n", h=2)
                    P.op("dve", lambda e: e.tensor_reduce(out=stt[:, 0:2], in_=Y3, axis=AX.X, op=ALU.add), reads=[r_Y], writes=[r_stt])
                    sqt = self.tmpf[0][:, 0:128]
                    P.op("dve", lambda e: e.tensor_tensor(out=sqt, in0=Ytok, in1=Ytok, op=ALU.mult), reads=[r_Y], writes=[self.r_tmpf[0]])
                    P.op("dve", lambda e: e.tensor_reduce(out=stt[:, 2:4], in_=sqt.rearrange("p (h n) -> p h n", h=2), axis=AX.X, op=ALU.add), reads=[self.r_tmpf[0]], writes=[r_stt])
                    P.op("dve", lambda e: e.tensor_scalar(out=stt[:, 4:6], in0=stt[:, 0:2], scalar1=1.0 / 64, scalar2=None, op0=ALU.mult), reads=[r_stt], writes=[r_stt])
                    P.op("dve", lambda e: e.tensor_tensor(out=stt[:, 6:8], in0=stt[:, 4:6], in1=stt[:, 4:6], op=ALU.mult), reads=[r_stt], writes=[r_stt])
                    P.op("dve", lambda e: e.scalar_tensor_tensor(out=stt[:, 8:10], in0=stt[:, 2:4], scalar=1.0 / 64, in1=stt[:, 6:8], op0=ALU.mult, op1=ALU.subtract), reads=[r_stt], writes=[r_stt])
                    P.op("act", lambda e: e.activation(out=stt[:, 10:12], in_=stt[:, 8:10], func=AF.Sqrt, bias=self.V("eps_gn")), reads=[r_stt, self.r_vecs], writes=[r_stt])
                    P.op("dve", lambda e: e.reciprocal(out=stt[:, 12:14], in_=stt[:, 10:12]), reads=[r_stt], writes=[r_stt])
                    for hp in range(2):
                        P.op("dve", lambda e, hp=hp: e.tensor_scalar(out=yn[:, hp * 64:(hp + 1) * 64], in0=Ytok[:, hp * 64:(hp + 1) * 64], scalar1=stt[:, 4 + hp:5 + hp], scalar2=stt[:, 12 + hp:13 + hp],
                                                                     op0=ALU.subtract, op1=ALU.mult), reads=[r_Y, r_stt], writes=[r_yn])
                    pty = pb[7][:, 256:320].bitcast(BF16)
                    P.op("pe", lambda e: e.transpose(pty, yn, ident), reads=[r_yn, self.r_cm], writes=[self.r_pb[7]])
                    t1 = self.tmpf[1][:, 0:128]
                    P.op("dve", lambda e: e.tensor_scalar(out=t1, in0=pty, scalar1=self.V("gn_g", c, 1), scalar2=self.V("gn_b", c, 1), op0=ALU.mult, op1=ALU.add), reads=[self.r_pb[7], self.r_vecs], writes=[self.r_tmpf[1]])
                    P.op("dve", lambda e, sl=sl: e.tensor_tensor(out=t1, in0=t1, in1=F_B[:, sl], op=ALU.add), reads=[self.r_tmpf[1], r_FB], writes=[self.r_tmpf[1]])
                    P.op("dve", lambda e, sl=sl, gchunk=gchunk: e.tensor_tensor(out=bigg[:, c, gchunk, :], in0=t1, in1=O_zs[:, sl], op=ALU.mult), reads=[self.r_tmpf[1], r_O["zs"]], writes=[self.r_big[c][ti]])
                for q_ in range(4 if STAGE >= 2 else 0):
                    do_chunk(q_)
            for ti_ in tiles:
                do_tile(ti_)
            self.hout_toks.append(P.dma("sp", lambda e, c=c: e.dma_start(out=self.d_hout[c], in_=Hf), reads=r_H))
        for c_ in range(KC):
            do_block(c_)
        self.out_proj(tiles, lambda cc, ti: (bigg[:, cc, TILES[ti][0] // 128:(TILES[ti][0] + TILES[ti][1]) // 128, :], self.r_big[cc][ti]))
        self.barrier()


def _consts():
    cm = np.zeros((128, 8, 128), np.float32)
    cm[:, 0, :] = np.eye(128)
    cm[:, 1, :] = 1.0
    s = np.arange(128)[:, None]; t = np.arange(128)[None, :]
    cm[:, 2, :] = (s <= t)
    cm[:, 3, :] = ((s // 64) == (t // 64))
    cm[:, 4, :] = (s < t)
    cm[:, 5, :] = (t < s)
    return cm


def prepare(inp, nlayers):
    wst = build_wstream(inp, nlayers)
    x = inp["x"]
    per_core = []
    vec_off = None
    for cid in range(NCORES):
        b, sgm = cid // 4, cid % 4
        t0 = sgm * OWN
        xs = np.zeros((NT, D), np.float32)
        xs[HALO:] = x[b, t0:t0 + OWN]
        if sgm > 0:
            xs[:HALO] = x[b, t0 - HALO:t0]
        vp = VecPack()
        vp.add("c", _pv(inp["c"][b]))
        vp.add("eps_rms", np.full((128, 1), RMS_EPS, np.float32))
        vp.add("eps_ln", np.full((128, 1), LN_EPS, np.float32))
        vp.add("final_g", _pv(inp["final_norm_g"]))
        for i in range(N_LAYERS):
            vp.add("norm_g%d" % i, _pv(inp["norm_g"][i]))
            mb = inp["mod_b"][i]
            vp.add("mod_b%d" % i, np.concatenate([_pv(mb[0:D]), _pv(mb[D:2 * D]), _pv(mb[2 * D:3 * D])], axis=1))
        for jj in range(2):
            vp.add("sg_ln_g%d" % jj, _pv(inp["sg_ln_g"][jj]))
            vp.add("sg_ln_b%d" % jj, _pv(inp["sg_ln_b"][jj]))
        inv = (10000.0 ** (-np.arange(32, dtype=np.float32) / np.float32(32))).astype(np.float32)
        vp.add("inv_freq", np.tile(inv, 4)[:, None])
        sign = np.where((np.arange(128) % 64) < 32, -1.0, 1.0).astype(np.float32)
        vp.add("rope_sign", sign[:, None])
        vp.add("rope_nb", (-np.pi * sign)[:, None].astype(np.float32))
        vp.add("negpi", np.full((128, 1), -np.pi, np.float32))
        vp.add("sinks", np.tile(np.asarray(inp["swa_sinks"][0], np.float32)[None, :], (128, 1)))
        vp.add("hasprev", np.full((128, 1), 1.0 if sgm > 0 else 0.0, np.float32))
        vp.add("eps_gn", np.full((128, 1), GN_EPS, np.float32))
        mu = np.asarray(inp["rwkv_mu"][0], np.float32)
        for nm, o in (("mu_r", 0), ("mu_k", D), ("mu_v", 2 * D), ("mu_z", 3 * D)):
            vp.add(nm, _pv(mu[o:o + D]))
        for nm, o in (("mu_dw", 4 * D), ("mu_da", 4 * D + 96)):
            t_ = np.zeros((128, 1), np.float32); t_[:96, 0] = mu[o:o + 96]
            vp.add(nm, t_)
        for nm, key in (("w0", "rwkv_w0"), ("a0", "rwkv_a0"), ("k_k", "rwkv_k_k"), ("k_a", "rwkv_k_a"), ("r_k", "rwkv_r_k"), ("gn_g", "rwkv_gn_g"), ("gn_b", "rwkv_gn_b")):
            vp.add(nm, _pv(np.asarray(inp[key][0], np.float32).reshape(-1)))
        vec_off = vp.off
        pos = np.zeros((NT,), np.int32)
        pos[HALO:] = inp["positions"][b, t0:t0 + OWN]
        if sgm > 0:
            pos[:HALO] = inp["positions"][b, t0 - HALO:t0]
        qi = np.arange(128)[:, None]; kj = np.arange(128)[None, :]
        NEG = -30000.0
        std = np.concatenate([np.where(kj > qi, 0.0, NEG), np.where(kj <= qi, 0.0, NEG)], axis=1).astype(np.float32)
        nop = std.copy(); nop[:, :128] = NEG
        am = np.stack([std, nop, (nop if sgm == 0 else std)], axis=1)
        m = {"wlora": np.ascontiguousarray(np.stack([inp["rwkv_w_lora"][0], inp["rwkv_a_lora"][0]], axis=0)), "pos": pos, "amask": np.ascontiguousarray(am), "xT": np.ascontiguousarray(xs.T), "wst": wst, "vecs": vp.build(), "cmat": _consts(),
             "sgw": np.ascontiguousarray(inp["sg_w_spatial"].transpose(0, 3, 1, 2)).reshape(2, 128, 16 * 128),
             "sgb": np.ascontiguousarray(inp["sg_b_spatial"].reshape(2, 16 * 128))}
        per_core.append(m)
    return per_core, vec_off, wst.shape[0]


_CACHE = {}


def run(inp, nlayers=N_LAYERS, dbg=False):
    inp = {k: np.asarray(v) for k, v in inp.items()}
    in_maps, vec_off, nslots = prepare(inp, nlayers)
    key = (nlayers, dbg)
    if key not in _CACHE:
        _CACHE[key] = Builder(nlayers, nslots, vec_off, in_maps[0]["vecs"].shape[1], dbg).build()
    nc = _CACHE[key]
    eye_bd = np.eye(128, dtype=np.float32)
    for m in in_maps:
        m["pmt"] = np.ascontiguousarray(np.broadcast_to(eye_bd, (3, 16, 128, 128)))
        m["pn"] = np.zeros((3, 16, 128, 64), np.float32)
    res = run_bass_kernel_spmd(nc, in_maps, core_ids=list(range(NCORES)))
    if nlayers > 2 and not os.environ.get("RW_ONEPASS"):
        hout = [res.results[cid]["hout"] for cid in range(NCORES)]
        for cid in range(NCORES):
            b, sgm = cid // 4, cid % 4
            pmt = np.ascontiguousarray(np.broadcast_to(eye_bd, (3, 16, 128, 128))).copy()
            pn = np.zeros((3, 16, 128, 64), np.float32)
            for ip in range(sgm):
                ho = hout[b * 4 + ip]
                pn[ip] = ho[:, :, 0:64]
                mt = np.zeros((16, 128, 128), np.float32)
                for hp in range(2):
                    mt[:, hp * 64:(hp + 1) * 64, hp * 64:(hp + 1) * 64] = ho[:, hp * 64:(hp + 1) * 64, 64:128].transpose(0, 2, 1)
                pmt[ip] = mt
            in_maps[cid]["pmt"] = pmt
            in_maps[cid]["pn"] = pn
        res = run_bass_kernel_spmd(nc, in_maps, core_ids=list(range(NCORES)))
    out = np.zeros((2, 4096, D), np.float32)
    dbgs = []
    for cid in range(NCORES):
        b, sgm = cid // 4, cid % 4
        out[b, sgm * OWN:(sgm + 1) * OWN] = res.results[cid]["outT"].T
        if dbg:
            dbgs.append(res.results[cid]["dbgT"].T)
    return (out, dbgs) if dbg else out


def kernel(**inputs):
    return run(inputs)
```

```python
import contextlib
import os
import numpy as np
import concourse.bass as bass
import concourse.mybir as mybir
from concourse.bass_utils import run_bass_kernel_spmd

F32 = mybir.dt.float32
BF16 = mybir.dt.bfloat16
I32 = mybir.dt.int32
AF = mybir.ActivationFunctionType
ALU = mybir.AluOpType
AX = mybir.AxisListType

ENGS = ("pe", "act", "dve", "pool", "sp")
N_DMA_SEMS = 12
SAME_ENG_SYNC = True

D = 2048
KC = 16
OWN = 1024
HALO = 128
NT = OWN + HALO
NCORES = 8
TILES = [(0, 128), (128, 512), (640, 512)]
N_LAYERS = 4
RMS_EPS = 1e-6
LN_EPS = 1e-5
GN_EPS = 64e-5
DECAY_SCALE = float(np.exp(-0.5))


class Res:
    __slots__ = ("name", "w", "r")

    def __init__(self, name=""):
        self.name = name
        self.w = None
        self.r = []


class Prog:
    def __init__(self, nc):
        self.nc = nc
        self.ops = {e: [] for e in ENGS}
        self.seq = {e: 0 for e in ENGS}
        self.known = {e: {} for e in ENGS}
        self.tok_know = {}
        self.dma_rr = {e: 0 for e in ENGS}
        self.dma_val = {}
        self.dma_last = {}
        self.last_tok = {}

    def _need(self, eng, toks):
        kn = self.known[eng]
        waits = {}
        for t in toks:
            if t is None:
                continue
            key, val = t
            if kn.get(key, 0) >= val:
                continue
            if waits.get(key, 0) < val:
                waits[key] = val
        out = []
        for key, val in waits.items():
            if kn.get(key, 0) >= val:
                continue
            out.append((key, val))
            tk = self.tok_know.get((key, val))
            if tk is not None:
                for k2, v2 in tk.items():
                    if kn.get(k2, 0) < v2:
                        kn[k2] = v2
            kn[key] = val
        return out

    def _deps(self, reads, writes):
        toks = []
        for r in reads:
            toks.append(r.w)
        for w in writes:
            toks.append(w.w)
            toks.extend(w.r)
        return toks

    def _commit(self, tok, reads, writes):
        for r in reads:
            r.r.append(tok)
            if len(r.r) > 64:
                r.r = r.r[-64:] if False else r.r
        for w in writes:
            w.w = tok
            w.r = []

    def op(self, eng, fn, reads=(), writes=(), extra=()):
        toks = [r.w for r in reads] + list(extra)
        for w in writes:
            toks += [t for t in [w.w] + w.r if t is not None and t[0] != eng]
        waits = self._need(eng, toks)
        self.seq[eng] += 1
        tok = (eng, self.seq[eng])
        if eng == "pe" or not SAME_ENG_SYNC:
            self.known[eng][eng] = self.seq[eng]
        tk = dict(self.known[eng])
        tk[eng] = self.seq[eng]
        self.tok_know[tok] = tk
        self.ops[eng].append((waits, fn, None))
        self._commit(tok, reads, writes)
        self.last_tok[eng] = tok
        return tok

    def dma(self, q, fn, reads=(), writes=(), extra=()):
        i = self.dma_rr[q]
        self.dma_rr[q] = (i + 1) % N_DMA_SEMS
        key = ("d", q, i)
        toks = [self.dma_last.get(key)] + self._deps(reads, writes) + list(extra)
        waits = self._need(q, toks)
        val = self.dma_val.get(key, 0) + 16
        self.dma_val[key] = val
        tok = (key, val)
        self.dma_last[key] = tok
        self.tok_know[tok] = dict(self.known[q])
        self.ops[q].append((waits, fn, key))
        self._commit(tok, reads, writes)
        return tok

    def coll(self, fn, reads=(), writes=()):
        key = ("c", "pool")
        toks = [self.dma_last.get(key)] + self._deps(reads, writes)
        waits = self._need("pool", toks)
        val = self.dma_val.get(key, 0) + 1
        self.dma_val[key] = val
        tok = (key, val)
        self.dma_last[key] = tok
        self.tok_know[tok] = dict(self.known["pool"])
        self.ops["pool"].append((waits, fn, key))
        self._commit(tok, reads, writes)
        return tok

    def wait_all(self, eng, toks):
        waits = self._need(eng, toks)
        self.ops[eng].append((waits, None, None))

    def emit(self):
        nc = self.nc
        with contextlib.ExitStack() as st:
            esem = {e: st.enter_context(nc.semaphore("s_" + e)) for e in ENGS}
            dsem = {}
            for q in ENGS:
                if any(o[2] is not None for o in self.ops[q]):
                    for i in range(N_DMA_SEMS):
                        dsem[("d", q, i)] = st.enter_context(nc.semaphore("d_%s_%d" % (q, i)))
            dsem[("c", "pool")] = st.enter_context(nc.semaphore("c_sem"))
            block = st.enter_context(nc.Block())

            def semof(key):
                return esem[key] if isinstance(key, str) else dsem[key]

            def run(eng_name, e):
                for waits, fn, dkey in self.ops[eng_name]:
                    for key, val in waits:
                        e.wait_ge(semof(key), val)
                    if fn is None:
                        continue
                    ins = fn(e)
                    if dkey is None:
                        ins.then_inc(esem[eng_name], 1)
                    elif dkey == ("c", "pool"):
                        ins.then_inc(dsem[dkey])
                    else:
                        ins.then_inc(dsem[dkey], 16)

            @block.tensor
            def _(e):
                run("pe", e)

            @block.scalar
            def _(e):
                run("act", e)

            @block.vector
            def _(e):
                run("dve", e)

            @block.gpsimd
            def _(e):
                run("pool", e)

            @block.sync
            def _(e):
                run("sp", e)


def _blocks(W):
    nb = W.shape[1] // 128
    return np.ascontiguousarray(W.reshape(KC, 128, nb, 128).transpose(2, 1, 0, 3)).reshape(nb, 128, D)


def _pv(v):
    return np.ascontiguousarray(np.asarray(v, np.float32).reshape(KC, 128).T)


class VecPack:
    def __init__(self):
        self.cols = []
        self.off = {}
        self.n = 0

    def add(self, name, arr):
        arr = np.asarray(arr, np.float32)
        assert arr.shape[0] == 128
        self.off[name] = (self.n, arr.shape[1])
        self.cols.append(arr)
        self.n += arr.shape[1]

    def build(self):
        return np.ascontiguousarray(np.concatenate(self.cols, axis=1))


def layer_kind(i):
    return i % 3, i // 3


def build_wstream(inp, nlayers):
    blks = []
    for i in range(nlayers):
        kind, j = layer_kind(i)
        blks.append(_blocks(inp["mod_w"][i]))
        if kind == 0:
            W = inp["sg_w_in"][j]
            blks.append(_blocks(W[:, 2048:4096]))
            u = _blocks(W[:, 0:2048])
            z = _blocks(W[:, 4096:6144])
            blks.append(np.stack([u, z], axis=1).reshape(32, 128, D))
            blks.append(_blocks(inp["sg_w_out"][j]))
        elif kind == 1:
            W = inp["swa_w_in"][j]
            blks.append(_blocks(W[:, 2048:2560]))
            q = _blocks(W[:, 0:2048])
            z = _blocks(W[:, 2560:4608])
            blks.append(np.stack([q, z], axis=1).reshape(32, 128, D))
            blks.append(_blocks(inp["swa_w_out"][j]))
        else:
            W = inp["rwkv_w_in"][j]
            lo = np.zeros((D, 256), np.float32)
            lo[:, 0:96] = W[:, 8192:8288]
            lo[:, 128:224] = W[:, 8288:8384]
            blks.append(_blocks(lo))
            r = _blocks(W[:, 0:2048]); k = _blocks(W[:, 2048:4096]); v = _blocks(W[:, 4096:6144]); z = _blocks(W[:, 6144:8192])
            blks.append(np.stack([k, v], axis=1).reshape(32, 128, D))
            blks.append(np.stack([r, k, v, z], axis=1).reshape(64, 128, D))
            blks.append(_blocks(inp["rwkv_w_out"][j]))
    a = np.concatenate(blks, axis=0)
    assert a.shape[0] % 2 == 0
    return np.ascontiguousarray(a.reshape(a.shape[0] // 2, 2, 128, D).transpose(0, 2, 1, 3)).reshape(a.shape[0] // 2, 128, 2 * D)


class Builder:
    def __init__(self, nlayers, nslots_total, vec_off, nvec, dbg):
        self.nlayers = nlayers
        self.vec_off = vec_off
        self.dbg = dbg
        nc = self.nc = bass.Bass("TRN2", target_bir_lowering=False)
        self.P = Prog(nc)
        self.d_xT = nc.dram_tensor("xT", [D, NT], F32, kind="ExternalInput").ap()
        self.d_w = nc.dram_tensor("wst", [nslots_total, 128, 2 * D], F32, kind="ExternalInput").ap()
        self.d_vec = nc.dram_tensor("vecs", [128, nvec], F32, kind="ExternalInput").ap()
        self.d_cm = nc.dram_tensor("cmat", [128, 8, 128], F32, kind="ExternalInput").ap()
        self.d_sgw = nc.dram_tensor("sgw", [2, 128, 16 * 128], F32, kind="ExternalInput").ap()
        self.d_sgb = nc.dram_tensor("sgb", [2, 16 * 128], F32, kind="ExternalInput").ap()
        self.d_pos = nc.dram_tensor("pos", [NT], I32, kind="ExternalInput").ap()
        self.d_msk = nc.dram_tensor("amask", [128, 3, 256], F32, kind="ExternalInput").ap()
        self.d_wlora = nc.dram_tensor("wlora", [2, 96, D], F32, kind="ExternalInput").ap()
        self.d_summ = nc.dram_tensor("summ_int", [16 * 128, 192], F32)
        self.d_gath = nc.dram_tensor("gath_int", [NCORES * 16 * 128, 192], F32)
        self.hout_toks = []
        self.dumps = {}
        self.d_out = nc.dram_tensor("outT", [D, OWN], F32, kind="ExternalOutput").ap()
        if dbg:
            self.d_dbg = nc.dram_tensor("dbgT", [D, NT], F32, kind="ExternalOutput").ap()
        self.nvec = nvec
        self.wslot = 0

    def dump(self, name, ap, res):
        if not self.dbg or name in self.dumps:
            return
        shp = [int(x) for x in ap.shape]
        d = self.nc.dram_tensor("dump_" + name, shp, ap.dtype, kind="ExternalOutput").ap()
        self.dumps[name] = d
        self.hout_toks.append(self.P.dma("sp", lambda e: e.dma_start(out=d, in_=ap), reads=list(res)))

    def V(self, name, c0=0, n=None):
        off, w = self.vec_off[name]
        if n is None:
            n = w - c0
        return self.vecs[:, off + c0: off + c0 + n]

    def next_w(self):
        P = self.P
        s = self.wslot
        self.wslot += 1
        r = s % len(self.wring)
        t, res = self.wring[r], self.wres[r]
        src = self.d_w[s]
        P.dma("pool", lambda e, t=t, src=src: e.dma_start(out=t[:], in_=src), writes=[res])
        return t[:].rearrange("p (b f) -> p b f", b=2), res

    def build(self):
        nc, P = self.nc, self.P
        with contextlib.ExitStack() as st:
            sb = lambda n, s, d: st.enter_context(nc.sbuf_tensor(n, s, d))
            ps = lambda n, s, d: st.enter_context(nc.psum_tensor(n, s, d))
            self.xT = sb("xT_sb", [128, KC, NT], F32)
            self.hT = sb("hT_sb", [128, KC, NT], BF16)
            self.big = sb("big_sb", [128, KC * NT], BF16)
            self.wring = [sb("wring%d" % i, [128, 2 * D], BF16) for i in range(2)]
            self.wres = [Res("wring%d" % i) for i in range(2)]
            self.vecs = sb("vecs_sb", [128, self.nvec], F32)
            self.cm = sb("cm_sb", [128, 8, 128], BF16)
            self.rstd = sb("rstd_sb", [128, NT], F32)
            self.sq = [sb("sq%d" % i, [128, 512], BF16) for i in range(2)]
            self.tmpf = [sb("tmpf%d" % i, [128, 512], F32) for i in range(3)]
            self.modv = sb("modv_sb", [128, 64], F32)
            self.condT = sb("condT_sb", [128, KC], BF16)
            self.small = sb("small_sb", [128, 64], F32)
            self.scr = sb("scr_sb", [128, 7808], F32)
            self.pb = [ps("pb%d" % i, [128, 512], F32) for i in range(8)]
            self.r_x = [[Res("x%d_%d" % (k, t)) for t in range(3)] for k in range(KC)]
            self.r_h = [[Res("h%d_%d" % (k, t)) for t in range(3)] for k in range(KC)]
            self.r_big = [[Res("big%d_%d" % (k, t)) for t in range(3)] for k in range(KC)]
            self.r_pb = [Res("pb%d" % i) for i in range(8)]
            self.r_vecs = Res("vecs"); self.r_cm = Res("cm"); self.r_rstd = [Res("rstd%d" % t) for t in range(3)]
            self.r_sq = [Res("sq0"), Res("sq1")]; self.r_tmpf = [Res("tf%d" % i) for i in range(3)]
            self.r_modv = Res("modv"); self.r_cond = Res("cond"); self.r_small = Res("small")
            self.r_scr = Res("scr")

            P.dma("sp", lambda e: e.dma_start(out=self.vecs[:], in_=self.d_vec), writes=[self.r_vecs])
            P.dma("pool", lambda e: e.dma_start(out=self.cm[:], in_=self.d_cm), writes=[self.r_cm])
            for k in range(KC):
                for ti, (ts, tn) in enumerate(TILES):
                    P.dma("sp", lambda e, k=k, ts=ts, tn=tn: e.dma_start(out=self.xT[:, k, ts:ts + tn], in_=self.d_xT[k * 128:(k + 1) * 128, ts:ts + tn]),
                          writes=[self.r_x[k][ti]])
            P.op("act", lambda e: e.activation(out=self.condT[:], in_=self.V("c"), func=AF.Silu), reads=[self.r_vecs], writes=[self.r_cond])

            for i in range(self.nlayers):
                kind, j = layer_kind(i)
                tiles = [0, 1, 2] if i < 2 else [1, 2]
                self.mod_and_norm(i, [0, 1, 2] if i <= 2 else tiles)
                if kind == 0:
                    self.sg_layer(i, j, tiles)
                elif kind == 1:
                    self.swa_layer(i, j, tiles)
                else:
                    self.rwkv_layer(i, j, tiles)
            outs = []
            if self.dbg:
                for k in range(KC):
                    for ti, (ts, tn) in enumerate(TILES):
                        outs.append(P.dma("sp", lambda e, k=k, ts=ts, tn=tn: e.dma_start(out=self.d_dbg[k * 128:(k + 1) * 128, ts:ts + tn], in_=self.xT[:, k, ts:ts + tn]),
                                          reads=[self.r_x[k][ti]]))
            outs += self.final_norm()
            outs += self.hout_toks
            P.wait_all("sp", outs)
            P.emit()
        return nc

    def mod_and_norm(self, i, tiles):
        P = self.P
        pmod = self.pb[7]
        for s in range(24):
            w, wres = self.next_w()
            for b in range(2):
                jb = 2 * s + b
                for kc in range(KC):
                    P.op("pe", lambda e, w=w, b=b, kc=kc, jb=jb: e.matmul(pmod[:, jb:jb + 1], w[:, b, kc * 128:(kc + 1) * 128], self.condT[:, kc:kc + 1],
                                                                          start=(kc == 0), stop=(kc == KC - 1)),
                         reads=[wres, self.r_cond], writes=[self.r_pb[7]])
        mv = self.modv
        P.op("dve", lambda e: e.tensor_tensor(out=mv[:, 0:48], in0=pmod[:, 0:48], in1=self.V("mod_b%d" % i), op=ALU.add),
             reads=[self.r_pb[7], self.r_vecs], writes=[self.r_modv])
        P.op("dve", lambda e: e.scalar_tensor_tensor(out=mv[:, 48:64], in0=mv[:, 16:32], scalar=1.0, in1=self.V("norm_g%d" % i), op0=ALU.add, op1=ALU.mult),
             reads=[self.r_modv, self.r_vecs], writes=[self.r_modv])
        self.rms_to_h(tiles, lambda kc: mv[:, 48 + kc:49 + kc], lambda kc: mv[:, kc:kc + 1], [self.r_modv])

    def rms_stats(self, tiles):
        P = self.P
        for ti in tiles:
            ts, tn = TILES[ti]
            pst = self.pb[6]
            for kc in range(KC):
                q = kc % 2
                P.op("act", lambda e, kc=kc, q=q, ts=ts, tn=tn: e.activation(out=self.sq[q][:, :tn], in_=self.xT[:, kc, ts:ts + tn], func=AF.Square),
                     reads=[self.r_x[kc][ti]], writes=[self.r_sq[q]])
                P.op("pe", lambda e, kc=kc, q=q, tn=tn: e.matmul(pst[:, :tn], self.cm[:, 1, :], self.sq[q][:, :tn], start=(kc == 0), stop=(kc == KC - 1)),
                     reads=[self.r_sq[q], self.r_cm], writes=[self.r_pb[6]])
            P.op("act", lambda e, ts=ts, tn=tn: e.activation(out=self.rstd[:, ts:ts + tn], in_=pst[:, :tn], func=AF.Sqrt, bias=self.V("eps_rms"), scale=1.0 / D),
                 reads=[self.r_pb[6], self.r_vecs], writes=[self.r_rstd[ti]])
            P.op("dve", lambda e, ts=ts, tn=tn: e.reciprocal(out=self.rstd[:, ts:ts + tn], in_=self.rstd[:, ts:ts + tn]),
                 reads=[self.r_rstd[ti]], writes=[self.r_rstd[ti]])

    def rms_to_h(self, tiles, A, Bsh, extra_reads):
        P = self.P
        self.rms_stats(tiles)
        for ti in tiles:
            ts, tn = TILES[ti]
            for kc in range(KC):
                q = kc % 3
                P.op("dve", lambda e, kc=kc, q=q, ts=ts, tn=tn: e.scalar_tensor_tensor(out=self.tmpf[q][:, :tn], in0=self.xT[:, kc, ts:ts + tn], scalar=A(kc),
                                                                                     in1=self.rstd[:, ts:ts + tn], op0=ALU.mult, op1=ALU.mult),
                     reads=[self.r_x[kc][ti], self.r_rstd[ti]] + extra_reads, writes=[self.r_tmpf[q]])
                P.op("act", lambda e, kc=kc, q=q, ts=ts, tn=tn: e.activation(out=self.hT[:, kc, ts:ts + tn], in_=self.tmpf[q][:, :tn], func=AF.Identity, bias=Bsh(kc)),
                     reads=[self.r_tmpf[q]] + extra_reads, writes=[self.r_h[kc][ti]])

    def final_norm(self):
        P = self.P
        tiles = [1, 2]
        self.rms_stats(tiles)
        outs = []
        for ti in tiles:
            ts, tn = TILES[ti]
            for kc in range(KC):
                q = kc % 3
                P.op("dve", lambda e, kc=kc, q=q, ts=ts, tn=tn: e.scalar_tensor_tensor(out=self.tmpf[q][:, :tn], in0=self.xT[:, kc, ts:ts + tn], scalar=self.V("final_g", kc, 1),
                                                                                     in1=self.rstd[:, ts:ts + tn], op0=ALU.mult, op1=ALU.mult),
                     reads=[self.r_x[kc][ti], self.r_rstd[ti], self.r_vecs], writes=[self.r_tmpf[q]])
                outs.append(P.dma("sp", lambda e, kc=kc, q=q, ts=ts, tn=tn: e.dma_start(out=self.d_out[kc * 128:(kc + 1) * 128, ts - HALO:ts - HALO + tn], in_=self.tmpf[q][:, :tn]),
                                  reads=[self.r_tmpf[q]]))
        return outs

    def out_proj(self, tiles, rhs_of):
        P = self.P
        nb = 0
        for s in range(8):
            w, wres = self.next_w()
            for ti in tiles:
                ts, tn = TILES[ti]
                for b in range(2):
                    m = 2 * s + b
                    pbi = nb % 4
                    nb += 1
                    po = self.pb[pbi]
                    for cc in range(KC):
                        rhs, rres = rhs_of(cc, ti)
                        P.op("pe", lambda e, po=po, w=w, b=b, cc=cc, rhs=rhs, tn=tn: e.matmul(po[:, :tn], w[:, b, cc * 128:(cc + 1) * 128], rhs, start=(cc == 0), stop=(cc == KC - 1)),
                             reads=[wres, rres], writes=[self.r_pb[pbi]])
                    P.op("dve", lambda e, po=po, m=m, ts=ts, tn=tn: e.scalar_tensor_tensor(out=self.xT[:, m, ts:ts + tn], in0=po[:, :tn], scalar=self.modv[:, 32 + m:33 + m],
                                                                                         in1=self.xT[:, m, ts:ts + tn], op0=ALU.mult, op1=ALU.add),
                         reads=[self.r_pb[pbi], self.r_modv, self.r_x[m][ti]], writes=[self.r_x[m][ti]])

    def sg_layer(self, i, j, tiles):
        P = self.P
        chunks = [c for ti in tiles for c in range(TILES[ti][0] // 128, (TILES[ti][0] + TILES[ti][1]) // 128)]
        tile_of_chunk = {c: ti for ti in tiles for c in range(TILES[ti][0] // 128, (TILES[ti][0] + TILES[ti][1]) // 128)}
        bigv = self.big[:].rearrange("p (i c) -> p i c", i=NT // 128)
        scr = self.scr
        wTm = scr[:, 0:1024].bitcast(BF16).rearrange("p (g t) -> p g t", g=16)
        bias2 = scr[:, 1024:3072].rearrange("p (g t) -> p g t", g=16)
        ssum = scr[:, 3072:3072 + 72].rearrange("p (i s) -> p i s", i=9)
        ssq = scr[:, 3200:3200 + 72].rearrange("p (i s) -> p i s", i=9)
        st4 = scr[:, 3328:3328 + 64]
        junk = scr[:, 3456:3456 + 128].bitcast(BF16)
        r_w = Res("sg_wTm"); r_b2 = Res("sg_bias2"); r_st = Res("sg_stats"); r_junk = Res("junk")
        P.dma("pool", lambda e: e.dma_start(out=wTm, in_=self.d_sgw[j].rearrange("p (g t) -> p g t", g=16)), writes=[r_w], reads=[self.r_scr])
        msk = self.cm[:, 2, :]
        mskb = bass.AP(msk.tensor, msk.offset, [msk.ap[0], [0, 16], msk.ap[1]])
        P.op("dve", lambda e: e.tensor_tensor(out=wTm, in0=wTm, in1=mskb, op=ALU.mult), reads=[r_w, self.r_cm], writes=[r_w])
        sgb = self.d_sgb[j]
        P.dma("sp", lambda e: e.dma_start(out=bias2, in_=bass.AP(sgb.tensor, sgb.offset, [[0, 128], [128, 16], [1, 128]])), writes=[r_b2], reads=[self.r_scr])
        P.op("dve", lambda e: e.memset(scr[:, 3072:3456], 0.0), writes=[r_st], reads=[self.r_scr])
        for g4 in range(4):
            pr = self.pb[4 + (g4 % 2)]
            for gg in range(4):
                g = g4 * 4 + gg
                P.op("pe", lambda e, pr=pr, g=g, gg=gg: e.matmul(pr[:, gg * 128:(gg + 1) * 128], self.cm[:, 1, :], wTm[:, g, :], start=True, stop=True),
                     reads=[r_w, self.r_cm], writes=[self.r_pb[4 + (g4 % 2)]])
            for gg in range(4):
                g = g4 * 4 + gg
                P.op("dve", lambda e, pr=pr, g=g, gg=gg: e.scalar_tensor_tensor(out=bias2[:, g, :], in0=pr[:, gg * 128:(gg + 1) * 128], scalar=self.V("sg_ln_b%d" % j, g, 1),
                                                                                 in1=bias2[:, g, :], op0=ALU.mult, op1=ALU.add),
                     reads=[self.r_pb[4 + (g4 % 2)], self.r_vecs, r_b2], writes=[r_b2])
        nb = 0
        for s in range(8):
            w, wres = self.next_w()
            for c in chunks:
                ti = tile_of_chunk[c]
                pbi = nb % 4
                nb += 1
                pv = self.pb[pbi]
                for kc in range(KC):
                    P.op("pe", lambda e, pv=pv, w=w, kc=kc, c=c: e.matmul(pv[:, 0:256].rearrange("p (b f) -> p b f", b=2), self.hT[:, kc, c * 128:(c + 1) * 128],
                                                                          w[:, :, kc * 128:(kc + 1) * 128], start=(kc == 0), stop=(kc == KC - 1)),
                         reads=[wres, self.r_h[kc][ti]], writes=[self.r_pb[pbi]])
                P.op("act", lambda e, pv=pv, c=c, s=s: e.activation(out=bigv[:, c, s * 256:(s + 1) * 256], in_=pv[:, 0:256], func=AF.Gelu_apprx_tanh, accum_out=ssum[:, c, s:s + 1]),
                     reads=[self.r_pb[pbi]], writes=[self.r_big[2 * s][ti], self.r_big[2 * s + 1][ti], r_st])
                P.op("act", lambda e, c=c, s=s: e.activation(out=junk, in_=bigv[:, c, s * 256:(s + 1) * 256], func=AF.Square, accum_out=ssq[:, c, s:s + 1]),
                     reads=[self.r_big[2 * s][ti], self.r_big[2 * s + 1][ti]], writes=[r_junk, r_st])
        for c in chunks:
            ti = tile_of_chunk[c]
            allbig = [self.r_big[k][ti] for k in range(KC)]
            P.op("dve", lambda e, c=c: e.tensor_reduce(out=st4[:, 0:1], in_=ssum[:, c, :], axis=AX.X, op=ALU.add), reads=[r_st], writes=[r_st])
            P.op("dve", lambda e, c=c: e.tensor_reduce(out=st4[:, 1:2], in_=ssq[:, c, :], axis=AX.X, op=ALU.add), reads=[r_st], writes=[r_st])
            P.op("dve", lambda e: e.tensor_scalar(out=st4[:, 2:3], in0=st4[:, 0:1], scalar1=1.0 / D, scalar2=None, op0=ALU.mult), reads=[r_st], writes=[r_st])
            P.op("dve", lambda e: e.tensor_tensor(out=st4[:, 3:4], in0=st4[:, 2:3], in1=st4[:, 2:3], op=ALU.mult), reads=[r_st], writes=[r_st])
            P.op("dve", lambda e: e.scalar_tensor_tensor(out=st4[:, 4:5], in0=st4[:, 1:2], scalar=1.0 / D, in1=st4[:, 3:4], op0=ALU.mult, op1=ALU.subtract),
                 reads=[r_st], writes=[r_st])
            P.op("act", lambda e: e.activation(out=st4[:, 5:6], in_=st4[:, 4:5], func=AF.Sqrt, bias=self.V("eps_ln"), scale=1.0), reads=[r_st, self.r_vecs], writes=[r_st])
            P.op("dve", lambda e: e.reciprocal(out=st4[:, 6:7], in_=st4[:, 5:6]), reads=[r_st], writes=[r_st])
            P.op("dve", lambda e, c=c: e.tensor_scalar(out=bigv[:, c, :], in0=bigv[:, c, :], scalar1=st4[:, 2:3], scalar2=st4[:, 6:7], op0=ALU.subtract, op1=ALU.mult),
                 reads=allbig + [r_st], writes=allbig)
        bigg = self.big[:].rearrange("p (i c t) -> p c i t", i=NT // 128, c=KC)
        b2b = lambda g, n: bass.AP(bias2[:, g, :].tensor, bias2[:, g, :].offset, [bias2[:, g, :].ap[0], [0, n], bias2[:, g, :].ap[1]])
        for c in range(KC):
            w, wres = self.next_w()
            for ti in tiles:
                ts, tn = TILES[ti]
                nch = tn // 128
                c0 = ts // 128
                pu, pz, pf = self.pb[0 + 3 * (ti % 2)], self.pb[1 + 3 * (ti % 2)], self.pb[2 + 3 * (ti % 2)]
                ru, rz, rf = self.r_pb[0 + 3 * (ti % 2)], self.r_pb[1 + 3 * (ti % 2)], self.r_pb[2 + 3 * (ti % 2)]
                for kc in range(KC):
                    P.op("pe", lambda e, pu=pu, w=w, kc=kc, ts=ts, tn=tn: e.matmul(pu[:, :tn], w[:, 0, kc * 128:(kc + 1) * 128], self.hT[:, kc, ts:ts + tn], start=(kc == 0), stop=(kc == KC - 1)),
                         reads=[wres, self.r_h[kc][ti]], writes=[ru])
                for kc in range(KC):
                    P.op("pe", lambda e, pz=pz, w=w, kc=kc, ts=ts, tn=tn: e.matmul(pz[:, :tn], w[:, 1, kc * 128:(kc + 1) * 128], self.hT[:, kc, ts:ts + tn], start=(kc == 0), stop=(kc == KC - 1)),
                         reads=[wres, self.r_h[kc][ti]], writes=[rz])
                for q in range(nch):
                    P.op("pe", lambda e, pf=pf, q=q, c=c, c0=c0: e.matmul(pf[:, q * 128:(q + 1) * 128], bigv[:, c0 + q, c * 128:(c + 1) * 128], wTm[:, c, :], start=True, stop=True),
                         reads=[self.r_big[c][ti], r_w], writes=[rf])
                t0, t1, t2 = self.tmpf
                P.op("act", lambda e, pu=pu, tn=tn: e.activation(out=t0[:, :tn], in_=pu[:, :tn], func=AF.Gelu_apprx_tanh), reads=[ru], writes=[self.r_tmpf[0]])
                P.op("act", lambda e, pz=pz, tn=tn: e.activation(out=t1[:, :tn], in_=pz[:, :tn], func=AF.Silu), reads=[rz], writes=[self.r_tmpf[1]])
                P.op("dve", lambda e, tn=tn: e.tensor_tensor(out=t0[:, :tn], in0=t0[:, :tn], in1=t1[:, :tn], op=ALU.mult), reads=[self.r_tmpf[0], self.r_tmpf[1]], writes=[self.r_tmpf[0]])
                P.op("dve", lambda e, pf=pf, tn=tn, nch=nch, c=c: e.scalar_tensor_tensor(out=t2[:, :tn].rearrange("p (i t) -> p i t", i=nch), in0=pf[:, :tn].rearrange("p (i t) -> p i t", i=nch),
                                                                                         scalar=self.V("sg_ln_g%d" % j, c, 1), in1=b2b(c, nch), op0=ALU.mult, op1=ALU.add),
                     reads=[rf, self.r_vecs, r_b2], writes=[self.r_tmpf[2]])
                P.op("dve", lambda e, tn=tn, nch=nch, c=c, c0=c0: e.tensor_tensor(out=bigg[:, c, c0:c0 + nch, :], in0=t2[:, :tn].rearrange("p (i t) -> p i t", i=nch),
                                                                                 in1=t0[:, :tn].rearrange("p (i t) -> p i t", i=nch), op=ALU.mult),
                     reads=[self.r_tmpf[0], self.r_tmpf[2]], writes=[self.r_big[c][ti]])
        self.out_proj(tiles, lambda cc, ti: (bigg[:, cc, TILES[ti][0] // 128:(TILES[ti][0] + TILES[ti][1]) // 128, :], self.r_big[cc][ti]))
        self.barrier()

    def barrier(self):
        P = self.P
        toks = [P.last_tok.get(e) for e in ("pe", "act", "dve")] + [v for k, v in P.dma_last.items() if k[1] != "pool"]
        P.op("dve", lambda e: e.memset(self.small[:, 63:64], 0.0), extra=toks, writes=[self.r_scr, self.r_small])
        P.op("act", lambda e: e.activation(out=self.small[:, 62:63], in_=self.small[:, 63:64], func=AF.Copy), reads=[self.r_small, self.r_scr], writes=[self.r_small])

    def swa_layer(self, i, j, tiles):
        P = self.P
        scr = self.scr
        NCH = NT // 128
        cosT = scr[:, 0:1152]
        sinS = scr[:, 1152:2304]
        kdup = scr[:, 2304:4608].bitcast(BF16).rearrange("p (g t) -> p g t", g=4)
        vtok = scr[:, 4608:5760].bitcast(BF16).rearrange("p (i c) -> p i c", i=NCH)
        amask = scr[:, 5760:6144].bitcast(BF16).rearrange("p (v k) -> p v k", v=3)
        qrot = scr[:, 6144:6720].bitcast(BF16)
        zs = scr[:, 6720:7296].bitcast(BF16)
        qsw = scr[:, 7296:7808]
        Pm = [scr[:, 7808:7936].bitcast(BF16), scr[:, 7936:8064 - 32].bitcast(BF16)] if False else None
        r_cos = Res("cos"); r_sin = Res("sin"); r_kd = [Res("kd%d" % g) for g in range(4)]; r_vt = [Res("vt%d" % c) for c in range(NCH)]
        r_am = Res("amask"); r_qr = [Res("qr%d" % t) for t in range(3)]; r_zs = [Res("zs%d" % t) for t in range(3)]; r_qsw = Res("qsw")
        Pm = [self.sq[0][:, 0:256], self.sq[1][:, 0:256]]; r_Pm = self.r_sq
        PT = [self.sq[0][:, 256:512].rearrange("p (k q) -> p k q", k=2), self.sq[1][:, 256:512].rearrange("p (k q) -> p k q", k=2)]
        r_PT = [Res("PT0"), Res("PT1")]
        otmp = self.rstd[:, 0:128]; r_ot = Res("otmp")
        P.dma("pool", lambda e: e.dma_start(out=amask, in_=self.d_msk), writes=[r_am], reads=[self.r_scr])
        for ti in tiles:
            ts, tn = TILES[ti]
            pi_ = self.tmpf[2][:, :tn].bitcast(I32)
            pos = self.d_pos
            P.dma("sp", lambda e, pi_=pi_, ts=ts, tn=tn: e.dma_start(out=pi_, in_=bass.AP(pos.tensor, pos.offset + ts, [[0, 128], [1, tn]])), writes=[self.r_tmpf[2]])
            a0 = self.tmpf[0][:, :tn]; a1 = self.tmpf[1][:, :tn]
            P.op("dve", lambda e, a0=a0, pi_=pi_: e.tensor_copy(out=a0, in_=pi_), reads=[self.r_tmpf[2]], writes=[self.r_tmpf[0]])
            C1 = 6.28125
            C2 = float(2 * np.pi - 6.28125)
            TWO_PI = float(2 * np.pi)
            PI = float(np.pi)
            rt = [self.r_tmpf[0], self.r_tmpf[1], self.r_tmpf[2]]
            P.op("dve", lambda e, a0=a0: e.tensor_scalar(out=a0, in0=a0, scalar1=self.V("inv_freq"), scalar2=None, op0=ALU.mult), reads=[rt[0], self.r_vecs], writes=[rt[0]])
            P.op("dve", lambda e, a0=a0, a1=a1: e.tensor_scalar(out=a1, in0=a0, scalar1=float(1.0 / (2 * np.pi)), scalar2=None, op0=ALU.mult), reads=[rt[0]], writes=[rt[1]])
            P.op("dve", lambda e, a1=a1, pi_=pi_: e.tensor_copy(out=pi_, in_=a1), reads=[rt[1]], writes=[rt[2]])
            P.op("dve", lambda e, a1=a1, pi_=pi_: e.tensor_copy(out=a1, in_=pi_), reads=[rt[2]], writes=[rt[1]])
            P.op("dve", lambda e, a0=a0, a1=a1: e.scalar_tensor_tensor(out=a0, in0=a1, scalar=-C1, in1=a0, op0=ALU.mult, op1=ALU.add), reads=[rt[0], rt[1]], writes=[rt[0]])
            P.op("dve", lambda e, a0=a0, a1=a1: e.scalar_tensor_tensor(out=a0, in0=a1, scalar=-C2, in1=a0, op0=ALU.mult, op1=ALU.add), reads=[rt[0], rt[1]], writes=[rt[0]])
            a2 = self.tmpf[2][:, :tn]
            P.op("dve", lambda e, a0=a0, a1=a1: e.tensor_scalar(out=a1, in0=a0, scalar1=PI, scalar2=TWO_PI, op0=ALU.is_gt, op1=ALU.mult), reads=[rt[0]], writes=[rt[1]])
            P.op("dve", lambda e, a0=a0, a1=a1: e.tensor_tensor(out=a0, in0=a0, in1=a1, op=ALU.subtract), reads=[rt[0], rt[1]], writes=[rt[0]])
            P.op("dve", lambda e, a0=a0, a1=a1: e.tensor_scalar(out=a1, in0=a0, scalar1=-PI, scalar2=TWO_PI, op0=ALU.is_lt, op1=ALU.mult), reads=[rt[0]], writes=[rt[1]])
            P.op("dve", lambda e, a0=a0, a1=a1: e.tensor_tensor(out=a0, in0=a0, in1=a1, op=ALU.add), reads=[rt[0], rt[1]], writes=[rt[0]])
            P.op("act", lambda e, a0=a0, ts=ts, tn=tn: e.activation(out=sinS[:, ts:ts + tn], in_=a0, func=AF.Sin, scale=self.V("rope_sign")),
                 reads=[rt[0], self.r_vecs, self.r_scr], writes=[r_sin])
            P.op("dve", lambda e, a0=a0, a2=a2: e.tensor_scalar(out=a2, in0=a0, scalar1=float(np.pi / 2), scalar2=None, op0=ALU.add), reads=[rt[0]], writes=[rt[2]])
            P.op("dve", lambda e, a2=a2, a1=a1: e.tensor_scalar(out=a1, in0=a2, scalar1=PI, scalar2=TWO_PI, op0=ALU.is_gt, op1=ALU.mult), reads=[rt[2]], writes=[rt[1]])
            P.op("dve", lambda e, a2=a2, a1=a1: e.tensor_tensor(out=a2, in0=a2, in1=a1, op=ALU.subtract), reads=[rt[2], rt[1]], writes=[rt[2]])
            P.op("act", lambda e, a2=a2, ts=ts, tn=tn: e.activation(out=cosT[:, ts:ts + tn], in_=a2, func=AF.Sin), reads=[rt[2], self.r_scr], writes=[r_cos])

        def rope(psrc, rsrc, ts, tn, out_ap, out_res, dup=None):
            for (a, b) in ((0, 32), (32, 0), (64, 96), (96, 64)):
                eng = "act" if a in (0, 64) else "dve"
                if eng == "act":
                    P.op("act", lambda e, a=a, b=b: e.activation(out=qsw[a:a + 32, :tn], in_=psrc[b:b + 32, :tn], func=AF.Copy), reads=[rsrc], writes=[r_qsw])
                else:
                    P.op("dve", lambda e, a=a, b=b: e.tensor_copy(out=qsw[a:a + 32, :tn], in_=psrc[b:b + 32, :tn]), reads=[rsrc], writes=[r_qsw])
            t0 = self.tmpf[0][:, :tn]
            P.op("dve", lambda e: e.tensor_tensor(out=t0, in0=psrc[:, :tn], in1=cosT[:, ts:ts + tn], op=ALU.mult), reads=[rsrc, r_cos], writes=[self.r_tmpf[0]])
            P.op("dve", lambda e: e.tensor_tensor(out=qsw[:, :tn], in0=qsw[:, :tn], in1=sinS[:, ts:ts + tn], op=ALU.mult), reads=[r_qsw, r_sin], writes=[r_qsw])
            P.op("dve", lambda e: e.tensor_tensor(out=out_ap, in0=t0, in1=qsw[:, :tn], op=ALU.add), reads=[self.r_tmpf[0], r_qsw], writes=out_res)
            if dup is not None:
                dup()

        w, wres = self.next_w()
        for b in range(2):
            for ti in tiles:
                ts, tn = TILES[ti]
                pk = self.pb[ti % 2]; rk = self.r_pb[ti % 2]
                for kc in range(KC):
                    P.op("pe", lambda e, pk=pk, w=w, b=b, kc=kc, ts=ts, tn=tn: e.matmul(pk[:, :tn], w[:, b, kc * 128:(kc + 1) * 128], self.hT[:, kc, ts:ts + tn], start=(kc == 0), stop=(kc == KC - 1)),
                         reads=[wres, self.r_h[kc][ti]], writes=[rk])
                g0, g1 = 2 * b, 2 * b + 1
                ktmp = self.tmpf[1][:, :tn]

                def dup(ts=ts, tn=tn, g0=g0, g1=g1, ktmp=ktmp):
                    P.op("act", lambda e: e.activation(out=kdup[0:64, g0, ts:ts + tn], in_=ktmp[0:64, :], func=AF.Copy), reads=[self.r_tmpf[1]], writes=[r_kd[g0]])
                    P.op("act", lambda e: e.activation(out=kdup[64:128, g0, ts:ts + tn], in_=ktmp[0:64, :], func=AF.Copy), reads=[self.r_tmpf[1]], writes=[r_kd[g0]])
                    P.op("dve", lambda e: e.tensor_copy(out=kdup[0:64, g1, ts:ts + tn], in_=ktmp[64:128, :]), reads=[self.r_tmpf[1]], writes=[r_kd[g1]])
                    P.op("dve", lambda e: e.tensor_copy(out=kdup[64:128, g1, ts:ts + tn], in_=ktmp[64:128, :]), reads=[self.r_tmpf[1]], writes=[r_kd[g1]])
                rope(pk, rk, ts, tn, ktmp, [self.r_tmpf[1]], dup)
        w, wres = self.next_w()
        chunks = [c for ti in tiles for c in range(TILES[ti][0] // 128, (TILES[ti][0] + TILES[ti][1]) // 128)]
        tile_of_chunk = {c: ti for ti in tiles for c in range(TILES[ti][0] // 128, (TILES[ti][0] + TILES[ti][1]) // 128)}
        for c in chunks:
            ti = tile_of_chunk[c]
            pv = self.pb[2 + c % 2]; rv = self.r_pb[2 + c % 2]
            for kc in range(KC):
                P.op("pe", lambda e, pv=pv, w=w, kc=kc, c=c: e.matmul(pv[:, 0:256].rearrange("p (b f) -> p b f", b=2), self.hT[:, kc, c * 128:(c + 1) * 128],
                                                                      w[:, :, kc * 128:(kc + 1) * 128], start=(kc == 0), stop=(kc == KC - 1)),
                     reads=[wres, self.r_h[kc][ti]], writes=[rv])
            P.op("act", lambda e, pv=pv, c=c: e.activation(out=vtok[:, c, :], in_=pv[:, 0:256], func=AF.Copy), reads=[rv], writes=[r_vt[c]])

        bigg = self.big[:].rearrange("p (i c t) -> p c i t", i=NCH, c=KC)
        ptp = self.pb[6][:, 0:128].bitcast(BF16).rearrange("p (k q) -> p k q", k=2)
        unit = 0
        for c in range(KC):
            g = c // 4
            w, wres = self.next_w()
            for ti in tiles:
                ts, tn = TILES[ti]
                pq = self.pb[ti % 2]; rq = self.r_pb[ti % 2]
                pz = self.pb[2 + ti % 2]; rz = self.r_pb[2 + ti % 2]
                for kc in range(KC):
                    P.op("pe", lambda e, pq=pq, w=w, kc=kc, ts=ts, tn=tn: e.matmul(pq[:, :tn], w[:, 0, kc * 128:(kc + 1) * 128], self.hT[:, kc, ts:ts + tn], start=(kc == 0), stop=(kc == KC - 1)),
                         reads=[wres, self.r_h[kc][ti]], writes=[rq])
                for kc in range(KC):
                    P.op("pe", lambda e, pz=pz, w=w, kc=kc, ts=ts, tn=tn: e.matmul(pz[:, :tn], w[:, 1, kc * 128:(kc + 1) * 128], self.hT[:, kc, ts:ts + tn], start=(kc == 0), stop=(kc == KC - 1)),
                         reads=[wres, self.r_h[kc][ti]], writes=[rz])
                P.op("act", lambda e, pz=pz, ts=ts, tn=tn: e.activation(out=zs[:, ts:ts + tn], in_=pz[:, :tn], func=AF.Silu), reads=[rz], writes=[r_zs[ti]])
                rope(pq, rq, ts, tn, qrot[:, ts:ts + tn], [r_qr[ti]])
                for ci in range(ts // 128, (ts + tn) // 128):
                    halo = (ci == 0)
                    nkb = 1 if halo else 2
                    nk = 128 * nkb
                    k0 = 0 if halo else (ci - 1) * 128
                    var = 0 if ci > 1 else 2
                    mk = amask[:, 0, 128:256] if halo else amask[:, var, :]
                    po = self.pb[7]; rpo = self.r_pb[7]
                    for hh in range(2):
                        pp = hh * 64
                        hidx = 2 * c + hh
                        u = unit % 2
                        unit += 1
                        psc = self.pb[4 + u]; rps = self.r_pb[4 + u]
                        st = self.small[:, (unit % 6) * 8:(unit % 6) * 8 + 8]; r_st = self.r_small
                        sink = self.V("sinks", hidx, 1)
                        P.op("pe", lambda e, psc=psc, pp=pp, ci=ci, k0=k0, nk=nk, g=g: e.matmul(psc[:, :nk], qrot[pp:pp + 64, ci * 128:(ci + 1) * 128], kdup[pp:pp + 64, g, k0:k0 + nk], start=True, stop=False),
                             reads=[r_qr[ti], r_kd[g]], writes=[rps])
                        P.op("pe", lambda e, psc=psc, nk=nk, mk=mk: e.matmul(psc[:, :nk], self.cm[:, 0, :], mk, start=False, stop=True), reads=[self.r_cm, r_am], writes=[rps])
                        P.op("dve", lambda e, psc=psc, nk=nk, st=st: e.reduce_max(out=st[:, 0:1], in_=psc[:, :nk], axis=AX.X), reads=[rps], writes=[r_st])
                        P.op("dve", lambda e, st=st, sink=sink: e.tensor_scalar(out=st[:, 1:2], in0=st[:, 0:1], scalar1=0.125, scalar2=sink, op0=ALU.mult, op1=ALU.max), reads=[r_st, self.r_vecs], writes=[r_st])
                        P.op("dve", lambda e, st=st: e.tensor_scalar(out=st[:, 2:3], in0=st[:, 1:2], scalar1=-1.0, scalar2=None, op0=ALU.mult), reads=[r_st], writes=[r_st])
                        P.op("act", lambda e, psc=psc, nk=nk, st=st, u=u: e.activation(out=Pm[u][:, :nk], in_=psc[:, :nk], func=AF.Exp, bias=st[:, 2:3], scale=0.125, accum_out=st[:, 3:4]),
                             reads=[rps, r_st], writes=[r_Pm[u], r_st])
                        P.op("act", lambda e, st=st, sink=sink: e.activation(out=st[:, 4:5], in_=sink, func=AF.Exp, bias=st[:, 2:3], scale=1.0), reads=[r_st, self.r_vecs], writes=[r_st])
                        P.op("dve", lambda e, st=st: e.tensor_tensor(out=st[:, 5:6], in0=st[:, 3:4], in1=st[:, 4:5], op=ALU.add), reads=[r_st], writes=[r_st])
                        P.op("dve", lambda e, st=st: e.reciprocal(out=st[:, 6:7], in_=st[:, 5:6]), reads=[r_st], writes=[r_st])
                        P.op("dve", lambda e, st=st, u=u, nk=nk: e.tensor_scalar(out=Pm[u][:, :nk], in0=Pm[u][:, :nk], scalar1=st[:, 6:7], scalar2=None, op0=ALU.mult), reads=[r_st, r_Pm[u]], writes=[r_Pm[u]])
                        for kb in range(nkb):
                            P.op("pe", lambda e, u=u, kb=kb: e.transpose(ptp[:, kb, :], Pm[u][:, kb * 128:(kb + 1) * 128], self.cm[:, 0, :]), reads=[r_Pm[u], self.r_cm], writes=[self.r_pb[6]])
                        P.op("act", lambda e, u=u, nkb=nkb: e.activation(out=PT[u][:, 0:nkb, :], in_=ptp[:, 0:nkb, :], func=AF.Copy), reads=[self.r_pb[6]], writes=[r_PT[u]])
                        for kb in range(nkb):
                            kc_ = ci if halo else ci - 1 + kb
                            P.op("pe", lambda e, u=u, kb=kb, kc_=kc_, hh=hh, g=g, nkb=nkb: e.matmul(po[0:64, hh * 128:(hh + 1) * 128], vtok[:, kc_, g * 64:(g + 1) * 64], PT[u][:, kb, :],
                                                                                                   start=(kb == 0), stop=(kb == nkb - 1)),
                                 reads=[r_PT[u], r_vt[kc_]], writes=[rpo])
                        if hh == 0:
                            P.op("dve", lambda e, c=c, ci=ci: e.tensor_tensor(out=bigg[0:64, c, ci, :], in0=po[0:64, 0:128], in1=zs[0:64, ci * 128:(ci + 1) * 128], op=ALU.mult),
                                 reads=[rpo, r_zs[ti]], writes=[self.r_big[c][ti]])
                        else:
                            P.op("act", lambda e: e.activation(out=otmp[64:128, :], in_=po[0:64, 128:256], func=AF.Copy), reads=[rpo], writes=[r_ot])
                            P.op("dve", lambda e, c=c, ci=ci: e.tensor_tensor(out=bigg[64:128, c, ci, :], in0=otmp[64:128, :], in1=zs[64:128, ci * 128:(ci + 1) * 128], op=ALU.mult),
                                 reads=[r_ot, r_zs[ti]], writes=[self.r_big[c][ti]])
        self.out_proj(tiles, lambda cc, ti: (bigg[:, cc, TILES[ti][0] // 128:(TILES[ti][0] + TILES[ti][1]) // 128, :], self.r_big[cc][ti]))
        self.barrier()

    def rwkv_layer(self, i, j, tiles):
        P = self.P
        scr = self.scr
        NCH = NT // 128
        self.barrier()
        arenas = [[scr, 0, 7808], [self.rstd, 0, NT], [self.big[:, 0:2048].bitcast(F32), 0, 1024], [self.tmpf[2], 0, 512],
                  [self.sq[0][:].bitcast(F32), 0, 256], [self.sq[1][:].bitcast(F32), 0, 256],
                  [self.tmpf[0][:, 128:512], 0, 384], [self.tmpf[1][:, 128:512], 0, 384]]

        def alloc(n, dt=F32):
            for ar in arenas:
                if ar[1] + n <= ar[2]:
                    a = ar[0][:, ar[1]:ar[1] + n]
                    ar[1] += n
                    return a if dt == F32 else a.bitcast(BF16)
            raise AssertionError("rwkv scratch exhausted")
        TN = 512
        pf = alloc(516)
        F_L = alloc(TN); F_A = alloc(TN); F_K = alloc(TN); F_T = alloc(TN); F_R = alloc(TN); F_B = alloc(TN)
        E_pos = alloc(256, BF16); E_neg = alloc(256, BF16); E_end = alloc(256, BF16); E_prev = alloc(256, BF16)
        O_rt = alloc(256, BF16); O_kk = alloc(256, BF16); O_kh = alloc(256, BF16); O_bh = alloc(256, BF16)
        O_kb = alloc(256, BF16); O_bb = alloc(256, BF16); O_zs = alloc(256, BF16); O_v = alloc(256, BF16); O_t = alloc(256, BF16)
        Vaug = alloc(512, BF16).rearrange("p (c h n) -> p c h n", c=4, h=2)
        Kbt = alloc(256, BF16).rearrange("p (c n) -> p c n", c=4)
        Bbt = alloc(256, BF16).rearrange("p (c n) -> p c n", c=4)
        dwl = alloc(512, BF16); dal = alloc(512, BF16)
        wlo = alloc(64, BF16); alo = alloc(64, BF16)
        identf = alloc(128)
        r_idf = Res("identf"); r_summ = Res("summ"); r_gath = Res("gath")
        Hf = alloc(128); Hb = alloc(64, BF16)
        gC = alloc(8); carry = alloc(8); stt = alloc(16)
        Ytok = alloc(128)
        mats = {}
        for hp in range(2):
            for nm in ("AkT", "ArkT", "ArbT", "Xa0", "Xb0", "Xa1", "Xb1", "G0", "G1", "W", "Un"):
                mats[(nm, hp)] = (alloc(64, BF16), Res(nm + str(hp)))
        yn = alloc(64, BF16); r_yn = Res("yn")
        R = lambda n: Res(n)
        r_pf = R("pf"); r_FL = R("FL"); r_FA = R("FA"); r_FK = R("FK"); r_FT = R("FT"); r_FR = R("FR"); r_FB = R("FB")
        r_E = R("E"); r_O = {n: R(n) for n in ("rt", "kk", "kh", "bh", "kb", "bb", "zs", "v", "t")}
        r_Va = R("Vaug"); r_Kbt = R("Kbt"); r_Bbt = R("Bbt"); r_dwl = R("dwl"); r_dal = R("dal"); r_lo = R("lora")
        r_H = [R("H0"), R("H1")]; r_gC = R("gC"); r_carry = R("carry"); r_stt = R("stt"); r_Y = R("Ytok")
        r_q5 = [R("q5_%d" % q) for q in range(4)]; r_q6 = [R("q6_%d" % q) for q in range(4)]
        cnt = {"q5": 0, "q6": 0}
        pb = self.pb
        ident = self.cm[:, 0, :]; ones_blk = self.cm[:, 3, :]; m_u = self.cm[:, 2, :]; m_su = self.cm[:, 4, :]; m_sl = self.cm[:, 5, :]
        OWN0 = HALO
        P.dma("sp", lambda e: e.dma_start(out=identf, in_=self.d_cm[:, 0, :]), writes=[r_idf], reads=[self.r_scr])
        P.op("dve", lambda e: e.memset(Vaug, 0.0), writes=[r_Va], reads=[self.r_scr])

        def stage(pi, w, wres, b, ti, vec_mu, extra_flagcol=None):
            ts, tn = TILES[ti]
            pp_, rp = pb[pi], self.r_pb[pi]
            for kc in range(KC):
                P.op("pe", lambda e, kc=kc: e.matmul(pp_[:, :tn], w[:, b, kc * 128:(kc + 1) * 128], self.hT[:, kc, ts:ts + tn], start=(kc == 0), stop=(kc == KC - 1)),
                     reads=[wres, self.r_h[kc][ti]], writes=[rp])
            return pp_, rp

        def prev_col(pi, w, wres, b, slot):
            pp_, rp = pb[4], self.r_pb[4]
            for kc in range(KC):
                P.op("pe", lambda e, kc=kc: e.matmul(pp_[:, slot:slot + 1], w[:, b, kc * 128:(kc + 1) * 128], self.hT[:, kc, 127:128], start=(kc == 0), stop=(kc == KC - 1)),
                     reads=[wres, self.r_h[kc][0]], writes=[rp])
            P.op("dve", lambda e: e.tensor_tensor(out=carry[:, slot:slot + 1], in0=pp_[:, slot:slot + 1], in1=self.V("hasprev"), op=ALU.mult),
                 reads=[rp, self.r_vecs], writes=[r_carry])

        def lerp(pp_, rp, tn, slot, mu, out, r_out, act=None):
            P.op("act", lambda e: e.activation(out=pf[:, 1:1 + tn], in_=pp_[:, :tn], func=AF.Copy), reads=[rp], writes=[r_pf])
            P.op("dve", lambda e: e.tensor_copy(out=pf[:, 0:1], in_=carry[:, slot:slot + 1]), reads=[r_carry, r_pf], writes=[r_pf])
            P.op("dve", lambda e: e.tensor_copy(out=carry[:, slot:slot + 1], in_=pf[:, tn:tn + 1]), reads=[r_pf], writes=[r_carry])
            P.op("dve", lambda e: e.tensor_tensor(out=out, in0=pf[:, 0:tn], in1=pf[:, 1:1 + tn], op=ALU.subtract), reads=[r_pf], writes=[r_out])
            P.op("dve", lambda e: e.scalar_tensor_tensor(out=out, in0=out, scalar=mu, in1=pf[:, 1:1 + tn], op0=ALU.mult, op1=ALU.add), reads=[r_pf, r_out, self.r_vecs], writes=[r_out])

        w, wres = self.next_w()
        for b, (dst, rdst, mu) in enumerate(((dwl, r_dwl, self.V("mu_dw")), (dal, r_dal, self.V("mu_da")))):
            prev_col(4, w, wres, b, 4 + b)
            for ti in tiles:
                ts, tn = TILES[ti]
                pp_, rp = stage(b, w, wres, b, ti, None)
                lerp(pp_, rp, tn, 4 + b, mu, F_T, r_FT)
                if b == 0:
                    P.op("act", lambda e, ts=ts, tn=tn, dst=dst: e.activation(out=dst[:, ts - OWN0:ts - OWN0 + tn], in_=F_T, func=AF.Tanh), reads=[r_FT], writes=[rdst])
                else:
                    P.op("act", lambda e, ts=ts, tn=tn, dst=dst: e.activation(out=dst[:, ts - OWN0:ts - OWN0 + tn], in_=F_T, func=AF.Copy), reads=[r_FT], writes=[rdst])

        bigg = self.big[:].rearrange("p (i c t) -> p c i t", i=NCH, c=KC)

        def q5():
            b_ = (5, 0, 1)[cnt["q5"] % 3]
            cnt["q5"] += 1
            return pb[b_][:, 0:128], self.r_pb[b_]

        def q6():
            b_ = (6, 2, 3)[cnt["q6"] % 3]
            cnt["q6"] += 1
            return pb[b_][:, 0:128], self.r_pb[b_]

        STAGE = int(os.environ.get("RW_STAGE", "9"))
        def do_block(c, summary):
            if summary:
                w1, w1res = self.next_w()
                kw = (w1, w1res, 0); vw = (w1, w1res, 1)
                prev_col(4, w1, w1res, 0, 1); prev_col(4, w1, w1res, 1, 2)
            else:
                w1, w1res = self.next_w()
                w2, w2res = self.next_w()
                kw = (w1, w1res, 1); vw = (w2, w2res, 0)
                prev_col(4, w1, w1res, 0, 0); prev_col(4, w1, w1res, 1, 1); prev_col(4, w2, w2res, 0, 2); prev_col(4, w2, w2res, 1, 3)
            P.dma("pool", lambda e, c=c: e.dma_start(out=wlo[0:96, :], in_=self.d_wlora[0, :, c * 128:(c + 1) * 128]), writes=[r_lo], reads=[self.r_scr])
            P.dma("pool", lambda e, c=c: e.dma_start(out=alo[0:96, :], in_=self.d_wlora[1, :, c * 128:(c + 1) * 128]), writes=[r_lo], reads=[self.r_scr])
            P.op("dve", lambda e: e.memset(Hf, 0.0), writes=r_H, reads=[self.r_scr])
            P.op("dve", lambda e: e.memset(Hb, 0.0), writes=r_H, reads=r_H)
            if not summary:
                gb = pf[:, 0:192]; tI = pf[:, 192:320]; mte = pf[:, 320:384].bitcast(BF16)
                for rk in range(NCORES):
                    row0 = (rk * 16 + c) * 128
                    P.dma("pool", lambda e, row0=row0: e.dma_start(out=gb, in_=self.d_gath[row0:row0 + 128, :]), reads=[r_gath], writes=[r_pf])
                    P.op("dve", lambda e, rk=rk: e.tensor_scalar(out=tI, in0=identf, scalar1=self.V("pinval", rk, 1), scalar2=None, op0=ALU.mult), reads=[r_idf, self.r_vecs, r_pf], writes=[r_pf])
                    P.op("dve", lambda e, rk=rk: e.scalar_tensor_tensor(out=mte, in0=gb[:, 64:192], scalar=self.V("pvalid", rk, 1), in1=tI, op0=ALU.mult, op1=ALU.add), reads=[r_pf, self.r_vecs], writes=[r_pf])
                    psf, rpsf = q6()
                    P.op("pe", lambda e, psf=psf: e.matmul(psf[:, 0:64], mte, Hb[:, 0:64], start=True, stop=True), reads=[r_pf] + r_H, writes=[rpsf])
                    P.op("dve", lambda e, psf=psf, rk=rk: e.scalar_tensor_tensor(out=Hf[:, 0:64], in0=gb[:, 0:64], scalar=self.V("pvalid", rk, 1), in1=psf[:, 0:64], op0=ALU.mult, op1=ALU.add),
                         reads=[rpsf, r_pf, self.r_vecs] + r_H, writes=r_H)
                    P.op("act", lambda e: e.activation(out=Hb[:, 0:64], in_=Hf[:, 0:64], func=AF.Copy), reads=r_H, writes=r_H)
            P.op("dve", lambda e: e.tensor_copy(out=Hf[0:64, 64:128], in_=identf[0:64, 0:64]), reads=[r_idf] + r_H, writes=r_H)
            P.op("dve", lambda e: e.tensor_copy(out=Hf[64:128, 64:128], in_=identf[64:128, 64:128]), reads=[r_idf] + r_H, writes=r_H)
            P.op("act", lambda e: e.activation(out=Hb, in_=Hf, func=AF.Copy), reads=r_H, writes=r_H)
            def do_tile(ti):
                ts, tn = TILES[ti]
                t0 = ts - OWN0
                cc = slice(0, 128)
                pl, rpl = pb[4], self.r_pb[4]
                P.op("pe", lambda e, t0=t0, tn=tn: e.matmul(pl[:, :tn], wlo[0:96, cc], dwl[0:96, t0:t0 + tn], start=True, stop=True), reads=[r_lo, r_dwl], writes=[rpl])
                P.op("act", lambda e: e.activation(out=F_L, in_=pl[:, :tn], func=AF.Sigmoid, bias=self.V("w0", c, 1)), reads=[rpl, self.r_vecs], writes=[r_FL])
                P.op("pe", lambda e, t0=t0, tn=tn: e.matmul(pl[:, :tn], alo[0:96, cc], dal[0:96, t0:t0 + tn], start=True, stop=True), reads=[r_lo, r_dal], writes=[rpl])
                P.op("act", lambda e: e.activation(out=F_A, in_=pl[:, :tn], func=AF.Sigmoid, bias=self.V("a0", c, 1)), reads=[rpl, self.r_vecs], writes=[r_FA])
                P.op("dve", lambda e: e.tensor_scalar(out=F_L, in0=F_L, scalar1=-DECAY_SCALE, scalar2=None, op0=ALU.mult), reads=[r_FL], writes=[r_FL])
                for q in range(4):
                    sl = slice(q * 128, (q + 1) * 128)
                    P.op("dve", lambda e, sl=sl: e.tensor_tensor_scan(out=F_L[:, sl], data0=F_L[:, sl], data1=F_L[:, sl], initial=0.0, op0=ALU.add, op1=ALU.bypass), reads=[r_FL], writes=[r_FL])
                P.op("act", lambda e: e.activation(out=E_pos, in_=F_L, func=AF.Exp), reads=[r_FL], writes=[r_E])
                P.op("act", lambda e: e.activation(out=E_neg, in_=F_L, func=AF.Exp, scale=-1.0), reads=[r_FL], writes=[r_E])
                for q in range(4):
                    sl = slice(q * 128, (q + 1) * 128)
                    P.op("act", lambda e, sl=sl, q=q: e.activation(out=E_end[:, sl], in_=F_L[:, sl], func=AF.Exp, scale=-1.0, bias=F_L[:, q * 128 + 127:q * 128 + 128]), reads=[r_FL], writes=[r_E])
                    P.op("act", lambda e, q=q: e.activation(out=gC[:, q:q + 1], in_=F_L[:, q * 128 + 127:q * 128 + 128], func=AF.Exp), reads=[r_FL], writes=[r_gC])
                    P.op("dve", lambda e, sl=sl, q=q: e.tensor_copy(out=E_prev[:, q * 128 + 1:q * 128 + 128], in_=E_pos[:, q * 128:q * 128 + 127]), reads=[r_E], writes=[r_E])
                    P.op("dve", lambda e, q=q: e.memset(E_prev[:, q * 128:q * 128 + 1], 1.0), reads=[r_E], writes=[r_E])
                pp_, rp = stage(1, kw[0], kw[1], kw[2], ti, None)
                lerp(pp_, rp, tn, 1, self.V("mu_k", c, 1), F_K, r_FK)
                P.op("dve", lambda e: e.tensor_scalar(out=F_T, in0=F_K, scalar1=self.V("k_k", c, 1), scalar2=None, op0=ALU.mult), reads=[r_FK, self.r_vecs], writes=[r_FT])
                P.op("dve", lambda e: e.tensor_tensor(out=O_t, in0=F_T, in1=F_T, op=ALU.mult), reads=[r_FT], writes=[r_O["t"]])
                P.op("pe", lambda e: e.matmul(pl[:, :tn], ones_blk, O_t, start=True, stop=True), reads=[self.r_cm, r_O["t"]], writes=[rpl])
                P.op("act", lambda e: e.activation(out=F_B, in_=pl[:, :tn], func=AF.Sqrt), reads=[rpl], writes=[r_FB])
                P.op("dve", lambda e: e.tensor_scalar(out=F_B, in0=F_B, scalar1=1e-12, scalar2=None, op0=ALU.max), reads=[r_FB], writes=[r_FB])
                P.op("dve", lambda e: e.reciprocal(out=F_B, in_=F_B), reads=[r_FB], writes=[r_FB])
                P.op("dve", lambda e: e.tensor_tensor(out=F_T, in0=F_T, in1=F_B, op=ALU.mult), reads=[r_FT, r_FB], writes=[r_FT])
                P.op("dve", lambda e: e.tensor_tensor(out=O_kk, in0=F_T, in1=E_prev, op=ALU.mult), reads=[r_FT, r_E], writes=[r_O["kk"]])
                P.op("dve", lambda e: e.tensor_tensor(out=F_T, in0=F_T, in1=F_A, op=ALU.mult), reads=[r_FT, r_FA], writes=[r_FT])
                P.op("dve", lambda e: e.tensor_tensor(out=O_bh, in0=F_T, in1=E_neg, op=ALU.mult), reads=[r_FT, r_E], writes=[r_O["bh"]])
                P.op("dve", lambda e: e.tensor_tensor(out=O_bb, in0=F_T, in1=E_end, op=ALU.mult), reads=[r_FT, r_E], writes=[r_O["bb"]])
                P.op("dve", lambda e: e.tensor_scalar(out=F_B, in0=F_A, scalar1=self.V("k_a", c, 1), scalar2=self.V("k_a", c, 1), op0=ALU.mult, op1=ALU.subtract), reads=[r_FA, self.r_vecs], writes=[r_FB])
                P.op("dve", lambda e: e.scalar_tensor_tensor(out=F_K, in0=F_B, scalar=1.0, in1=F_K, op0=ALU.add, op1=ALU.mult), reads=[r_FB, r_FK], writes=[r_FK])
                P.op("dve", lambda e: e.tensor_tensor(out=O_kh, in0=F_K, in1=E_neg, op=ALU.mult), reads=[r_FK, r_E], writes=[r_O["kh"]])
                P.op("dve", lambda e: e.tensor_tensor(out=O_kb, in0=F_K, in1=E_end, op=ALU.mult), reads=[r_FK, r_E], writes=[r_O["kb"]])
                if not summary:
                    pp_, rp = stage(0, w1, w1res, 0, ti, None)
                    lerp(pp_, rp, tn, 0, self.V("mu_r", c, 1), F_R, r_FR)
                    P.op("dve", lambda e: e.scalar_tensor_tensor(out=O_t, in0=F_R, scalar=self.V("r_k", c, 1), in1=F_K, op0=ALU.mult, op1=ALU.mult), reads=[r_FR, r_FK, self.r_vecs], writes=[r_O["t"]])
                    P.op("pe", lambda e: e.matmul(pl[:, :tn], ones_blk, O_t, start=True, stop=True), reads=[self.r_cm, r_O["t"]], writes=[rpl])
                    P.op("dve", lambda e: e.tensor_tensor(out=O_rt, in0=F_R, in1=E_pos, op=ALU.mult), reads=[r_FR, r_E], writes=[r_O["rt"]])
                pp_, rp = stage(2, vw[0], vw[1], vw[2], ti, None)
                lerp(pp_, rp, tn, 2, self.V("mu_v", c, 1), F_R, r_FR)
                if not summary:
                    P.op("dve", lambda e: e.tensor_tensor(out=F_B, in0=pl[:, :tn], in1=F_R, op=ALU.mult), reads=[rpl, r_FR], writes=[r_FB])
                P.op("act", lambda e: e.activation(out=O_v, in_=F_R, func=AF.Copy), reads=[r_FR], writes=[r_O["v"]])
                if not summary:
                    pp_, rp = stage(3, w2, w2res, 1, ti, None)
                    lerp(pp_, rp, tn, 3, self.V("mu_z", c, 1), F_R, r_FR)
                    P.op("act", lambda e: e.activation(out=O_zs, in_=F_R, func=AF.Silu), reads=[r_FR], writes=[r_O["zs"]])
                if False:
                    for nm_, ap_, rs_ in (("FL", F_L, [r_FL]), ("FA", F_A, [r_FA]), ("k2", F_K, [r_FK]), ("b", F_T, [r_FT]), ("bonus", F_B, [r_FB]), ("Epos", E_pos, [r_E]), ("Eneg", E_neg, [r_E]),
                                          ("Eend", E_end, [r_E]), ("Eprev", E_prev, [r_E]), ("Ort", O_rt, [r_O["rt"]]), ("Okk", O_kk, [r_O["kk"]]), ("Okh", O_kh, [r_O["kh"]]),
                                          ("Obh", O_bh, [r_O["bh"]]), ("Okb", O_kb, [r_O["kb"]]), ("Obb", O_bb, [r_O["bb"]]), ("Ozs", O_zs, [r_O["zs"]]), ("Ov", O_v, [r_O["v"]]), ("gC", gC, [r_gC])):
                        self.dump(nm_, ap_, rs_)
                ptp = pb[7][:, 0:256].bitcast(BF16).rearrange("p (k q) -> p k q", k=4)
                for (src, rs, dst, rd, isv) in ((O_v, r_O["v"], None, r_Va, True), (O_kb, r_O["kb"], Kbt, r_Kbt, False), (O_bb, r_O["bb"], Bbt, r_Bbt, False)):
                    for q in range(4):
                        P.op("pe", lambda e, q=q, src=src: e.transpose(ptp[:, q, :], src[:, q * 128:(q + 1) * 128], ident), reads=[rs, self.r_cm], writes=[self.r_pb[7]])
                    if isv:
                        P.op("act", lambda e: e.activation(out=Vaug[:, :, :, 0:64], in_=ptp.rearrange("p c (h n) -> p c h n", h=2), func=AF.Copy), reads=[self.r_pb[7]], writes=[rd])
                    else:
                        P.op("act", lambda e, dst=dst: e.activation(out=dst, in_=ptp, func=AF.Copy), reads=[self.r_pb[7]], writes=[rd])
                def do_chunk(q):
                    sl = slice(q * 128, (q + 1) * 128)
                    gchunk = (ts // 128) + q
                    def do_head(hp):
                        p0 = hp * 64
                        M = lambda nm: mats[(nm, hp)]
                        rt = O_rt[p0:p0 + 64, sl]; kkt = O_kk[p0:p0 + 64, sl]; kh = O_kh[p0:p0 + 64, sl]; bh = O_bh[p0:p0 + 64, sl]
                        rr = [r_O["rt"], r_O["kk"], r_O["kh"], r_O["bh"]]

                        def mm_mask(lhs, rhs, mask, dst):
                            ps_, rps_ = q5()
                            P.op("pe", lambda e: e.matmul(ps_, lhs, rhs, start=True, stop=True), reads=rr, writes=[rps_])
                            P.op("dve", lambda e: e.tensor_tensor(out=dst[0], in0=ps_, in1=mask, op=ALU.mult), reads=[rps_, self.r_cm], writes=[dst[1]])
                        mm_mask(kh, kkt, m_su, M("AkT"))
                        if not summary:
                            mm_mask(kh, rt, m_u, M("ArkT"))
                            mm_mask(bh, rt, m_u, M("ArbT"))
                        for (lhs, rhs, mask, nm) in ((kkt, bh, m_sl, "Xa0"), (bh, kkt, m_su, "Xb0")):
                            ps_, rps_ = q5()
                            P.op("pe", lambda e, ps_=ps_, lhs=lhs, rhs=rhs: e.matmul(ps_, lhs, rhs, start=True, stop=True), reads=rr, writes=[rps_])
                            P.op("dve", lambda e, ps_=ps_, mask=mask, nm=nm: e.scalar_tensor_tensor(out=M(nm)[0], in0=ps_, scalar=-1.0, in1=mask, op0=ALU.mult, op1=ALU.mult),
                                 reads=[rps_, self.r_cm], writes=[M(nm)[1]])
                        P.op("dve", lambda e: e.tensor_tensor(out=M("G0")[0], in0=M("Xb0")[0], in1=ident, op=ALU.add), reads=[M("Xb0")[1], self.r_cm], writes=[M("G0")[1]])
                        cur = 0
                        for lvl in range(1, 7):
                            nxt = 1 - cur
                            Xa, Xb = M("Xa%d" % cur), M("Xb%d" % cur)
                            Xan, Xbn = M("Xa%d" % nxt), M("Xb%d" % nxt)
                            Gc, Gn = M("G%d" % cur), M("G%d" % nxt)
                            ps_, rps_ = q5()
                            P.op("pe", lambda e, ps_=ps_, Xa=Xa, Xb=Xb: e.matmul(ps_, Xb[0], Xa[0], start=True, stop=True), reads=[Xa[1], Xb[1]], writes=[rps_])
                            if lvl < 6:
                                ps2, rps2 = q5()
                                P.op("pe", lambda e, ps2=ps2, Xa=Xa, Xb=Xb: e.matmul(ps2, Xa[0], Xb[0], start=True, stop=True), reads=[Xa[1], Xb[1]], writes=[rps2])
                            P.op("act", lambda e, ps_=ps_, Xan=Xan: e.activation(out=Xan[0], in_=ps_, func=AF.Copy), reads=[rps_], writes=[Xan[1]])
                            if lvl < 6:
                                P.op("act", lambda e, ps2=ps2, Xbn=Xbn: e.activation(out=Xbn[0], in_=ps2, func=AF.Copy), reads=[rps2], writes=[Xbn[1]])
                            ps3, rps3 = q5()
                            P.op("pe", lambda e, ps3=ps3, Xan=Xan, Gc=Gc: e.matmul(ps3, Xan[0], Gc[0], start=True, stop=True), reads=[Xan[1], Gc[1]], writes=[rps3])
                            P.op("dve", lambda e, ps3=ps3, Gc=Gc, Gn=Gn: e.tensor_tensor(out=Gn[0], in0=ps3, in1=Gc[0], op=ALU.add), reads=[rps3, Gc[1]], writes=[Gn[1]])
                            cur = nxt
                        G = M("G%d" % cur)
                        if False:
                            for nm_ in ("AkT", "ArkT", "ArbT"):
                                self.dump(nm_, M(nm_)[0], [M(nm_)[1]])
                            self.dump("G", G[0], [G[1]])
                        Hh = Hb[p0:p0 + 64, :]
                        Va = Vaug[:, q, hp, :]
                        ps_, rps_ = q6()
                        P.op("pe", lambda e, ps_=ps_: e.matmul(ps_, kkt, Hh, start=True, stop=False), reads=rr + [r_H[hp]], writes=[rps_])
                        P.op("pe", lambda e, ps_=ps_: e.matmul(ps_, M("AkT")[0], Va, start=False, stop=True), reads=[M("AkT")[1], r_Va], writes=[rps_])
                        P.op("act", lambda e, ps_=ps_: e.activation(out=M("W")[0], in_=ps_, func=AF.Copy), reads=[rps_], writes=[M("W")[1]])
                        ps2, rps2 = q6()
                        P.op("pe", lambda e, ps2=ps2, G=G: e.matmul(ps2, G[0], M("W")[0], start=True, stop=True), reads=[G[1], M("W")[1]], writes=[rps2])
                        P.op("act", lambda e, ps2=ps2: e.activation(out=M("Un")[0], in_=ps2, func=AF.Copy, scale=-1.0), reads=[rps2], writes=[M("Un")[1]])
                        if not summary:
                            ps3, rps3 = q6()
                            P.op("pe", lambda e, ps3=ps3: e.matmul(ps3[:, 0:64], rt, Hh[:, 0:64], start=True, stop=False), reads=rr + [r_H[hp]], writes=[rps3])
                            P.op("pe", lambda e, ps3=ps3: e.matmul(ps3[:, 0:64], M("ArkT")[0], Va[:, 0:64], start=False, stop=False), reads=[M("ArkT")[1], r_Va], writes=[rps3])
                            P.op("pe", lambda e, ps3=ps3: e.matmul(ps3[:, 0:64], M("ArbT")[0], M("Un")[0][:, 0:64], start=False, stop=True), reads=[M("ArbT")[1], M("Un")[1]], writes=[rps3])
                            P.op("act", lambda e, ps3=ps3, p0=p0: e.activation(out=Ytok[:, p0:p0 + 64], in_=ps3[:, 0:64], func=AF.Copy), reads=[rps3], writes=[r_Y])
                        if False:
                            self.dump("W", M("W")[0], [M("W")[1]]); self.dump("Un", M("Un")[0], [M("Un")[1]]); self.dump("Ytok0", Ytok, [r_Y])
                        ps4, rps4 = q6()
                        ncol = 64 if hp == 0 else 128
                        P.op("pe", lambda e, ps4=ps4, ncol=ncol: e.matmul(ps4[0:ncol, :], Kbt[:, q, 0:ncol], Va, start=True, stop=False), reads=[r_Kbt, r_Va], writes=[rps4])
                        P.op("pe", lambda e, ps4=ps4, ncol=ncol: e.matmul(ps4[0:ncol, :], Bbt[:, q, 0:ncol], M("Un")[0], start=False, stop=True), reads=[r_Bbt, M("Un")[1]], writes=[rps4])
                        P.op("dve", lambda e, ps4=ps4, p0=p0, q=q: e.scalar_tensor_tensor(out=Hf[p0:p0 + 64, :], in0=Hf[p0:p0 + 64, :], scalar=gC[p0:p0 + 64, q:q + 1], in1=ps4[p0:p0 + 64, :],
                                                                                       op0=ALU.mult, op1=ALU.add), reads=[rps4, r_gC, r_H[hp]], writes=[r_H[hp]])
                        P.op("act", lambda e, p0=p0: e.activation(out=Hb[p0:p0 + 64, :], in_=Hf[p0:p0 + 64, :], func=AF.Copy), reads=[r_H[hp]], writes=[r_H[hp]])
                    for hp_ in range(2):
                        do_head(hp_)
                    if STAGE < 3 or summary:
                        return
                    Y3 = Ytok.rearrange("p (h n) -> p h n", h=2)
                    P.op("dve", lambda e: e.tensor_reduce(out=stt[:, 0:2], in_=Y3, axis=AX.X, op=ALU.add), reads=[r_Y], writes=[r_stt])
                    sqt = self.tmpf[0][:, 0:128]
                    P.op("dve", lambda e: e.tensor_tensor(out=sqt, in0=Ytok, in1=Ytok, op=ALU.mult), reads=[r_Y], writes=[self.r_tmpf[0]])
                    P.op("dve", lambda e: e.tensor_reduce(out=stt[:, 2:4], in_=sqt.rearrange("p (h n) -> p h n", h=2), axis=AX.X, op=ALU.add), reads=[self.r_tmpf[0]], writes=[r_stt])
                    P.op("dve", lambda e: e.tensor_scalar(out=stt[:, 4:6], in0=stt[:, 0:2], scalar1=1.0 / 64, scalar2=None, op0=ALU.mult), reads=[r_stt], writes=[r_stt])
                    P.op("dve", lambda e: e.tensor_tensor(out=stt[:, 6:8], in0=stt[:, 4:6], in1=stt[:, 4:6], op=ALU.mult), reads=[r_stt], writes=[r_stt])
                    P.op("dve", lambda e: e.scalar_tensor_tensor(out=stt[:, 8:10], in0=stt[:, 2:4], scalar=1.0 / 64, in1=stt[:, 6:8], op0=ALU.mult, op1=ALU.subtract), reads=[r_stt], writes=[r_stt])
                    P.op("act", lambda e: e.activation(out=stt[:, 10:12], in_=stt[:, 8:10], func=AF.Sqrt, bias=self.V("eps_gn")), reads=[r_stt, self.r_vecs], writes=[r_stt])
                    P.op("dve", lambda e: e.reciprocal(out=stt[:, 12:14], in_=stt[:, 10:12]), reads=[r_stt], writes=[r_stt])
                    for hp in range(2):
                        P.op("dve", lambda e, hp=hp: e.tensor_scalar(out=yn[:, hp * 64:(hp + 1) * 64], in0=Ytok[:, hp * 64:(hp + 1) * 64], scalar1=stt[:, 4 + hp:5 + hp], scalar2=stt[:, 12 + hp:13 + hp],
                                                                     op0=ALU.subtract, op1=ALU.mult), reads=[r_Y, r_stt], writes=[r_yn])
                    pty = pb[7][:, 256:320].bitcast(BF16)
                    P.op("pe", lambda e: e.transpose(pty, yn, ident), reads=[r_yn, self.r_cm], writes=[self.r_pb[7]])
                    t1 = self.tmpf[1][:, 0:128]
                    P.op("dve", lambda e: e.tensor_scalar(out=t1, in0=pty, scalar1=self.V("gn_g", c, 1), scalar2=self.V("gn_b", c, 1), op0=ALU.mult, op1=ALU.add), reads=[self.r_pb[7], self.r_vecs], writes=[self.r_tmpf[1]])
                    P.op("dve", lambda e, sl=sl: e.tensor_tensor(out=t1, in0=t1, in1=F_B[:, sl], op=ALU.add), reads=[self.r_tmpf[1], r_FB], writes=[self.r_tmpf[1]])
                    P.op("dve", lambda e, sl=sl, gchunk=gchunk: e.tensor_tensor(out=bigg[:, c, gchunk, :], in0=t1, in1=O_zs[:, sl], op=ALU.mult), reads=[self.r_tmpf[1], r_O["zs"]], writes=[self.r_big[c][ti]])
                for q_ in range(4 if STAGE >= 2 else 0):
                    do_chunk(q_)
            for ti_ in tiles:
                do_tile(ti_)
            if summary:
                sm = pf[:, 0:192]
                ptm = pb[7][:, 0:64].bitcast(BF16)
                P.op("pe", lambda e: e.transpose(ptm[0:64, :], Hb[:, 64:128], ident), reads=r_H + [self.r_cm], writes=[self.r_pb[7]])
                P.op("dve", lambda e: e.memset(sm[:, 64:192], 0.0), reads=[r_pf], writes=[r_pf])
                P.op("dve", lambda e: e.tensor_copy(out=sm[:, 0:64], in_=Hf[:, 0:64]), reads=r_H + [r_pf], writes=[r_pf])
                P.op("dve", lambda e: e.tensor_copy(out=sm[0:64, 64:128], in_=ptm[0:64, 0:64]), reads=[self.r_pb[7], r_pf], writes=[r_pf])
                P.op("act", lambda e: e.activation(out=sm[64:128, 128:192], in_=ptm[0:64, 64:128], func=AF.Copy), reads=[self.r_pb[7], r_pf], writes=[r_pf])
                P.dma("pool", lambda e, c=c: e.dma_start(out=self.d_summ[c * 128:(c + 1) * 128, :], in_=sm), reads=[r_pf], writes=[r_summ])
        for c_ in range(KC):
            do_block(c_, True)
        P.coll(lambda e: e.collective_compute("AllGather", ALU.bypass, replica_groups=[list(range(NCORES))], ins=[self.d_summ.ap().opt()], outs=[self.d_gath.ap().opt()]),
               reads=[r_summ], writes=[r_gath])
        for c_ in range(KC):
            do_block(c_, False)
        self.out_proj(tiles, lambda cc, ti: (bigg[:, cc, TILES[ti][0] // 128:(TILES[ti][0] + TILES[ti][1]) // 128, :], self.r_big[cc][ti]))
        self.barrier()


def _consts():
    cm = np.zeros((128, 8, 128), np.float32)
    cm[:, 0, :] = np.eye(128)
    cm[:, 1, :] = 1.0
    s = np.arange(128)[:, None]; t = np.arange(128)[None, :]
    cm[:, 2, :] = (s <= t)
    cm[:, 3, :] = ((s // 64) == (t // 64))
    cm[:, 4, :] = (s < t)
    cm[:, 5, :] = (t < s)
    return cm


def prepare(inp, nlayers):
    wst = build_wstream(inp, nlayers)
    x = inp["x"]
    per_core = []
    vec_off = None
    for cid in range(NCORES):
        b, sgm = cid // 4, cid % 4
        t0 = sgm * OWN
        xs = np.zeros((NT, D), np.float32)
        xs[HALO:] = x[b, t0:t0 + OWN]
        if sgm > 0:
            xs[:HALO] = x[b, t0 - HALO:t0]
        vp = VecPack()
        vp.add("c", _pv(inp["c"][b]))
        vp.add("eps_rms", np.full((128, 1), RMS_EPS, np.float32))
        vp.add("eps_ln", np.full((128, 1), LN_EPS, np.float32))
        vp.add("final_g", _pv(inp["final_norm_g"]))
        for i in range(N_LAYERS):
            vp.add("norm_g%d" % i, _pv(inp["norm_g"][i]))
            mb = inp["mod_b"][i]
            vp.add("mod_b%d" % i, np.concatenate([_pv(mb[0:D]), _pv(mb[D:2 * D]), _pv(mb[2 * D:3 * D])], axis=1))
        for jj in range(2):
            vp.add("sg_ln_g%d" % jj, _pv(inp["sg_ln_g"][jj]))
            vp.add("sg_ln_b%d" % jj, _pv(inp["sg_ln_b"][jj]))
        inv = (10000.0 ** (-np.arange(32, dtype=np.float32) / np.float32(32))).astype(np.float32)
        vp.add("inv_freq", np.tile(inv, 4)[:, None])
        sign = np.where((np.arange(128) % 64) < 32, -1.0, 1.0).astype(np.float32)
        vp.add("rope_sign", sign[:, None])
        vp.add("rope_nb", (-np.pi * sign)[:, None].astype(np.float32))
        vp.add("negpi", np.full((128, 1), -np.pi, np.float32))
        vp.add("sinks", np.tile(np.asarray(inp["swa_sinks"][0], np.float32)[None, :], (128, 1)))
        pval = np.array([1.0 if (rk // 4 == b and rk % 4 < sgm) else 0.0 for rk in range(NCORES)], np.float32)
        vp.add("pvalid", np.tile(pval[None, :], (128, 1)))
        vp.add("pinval", np.tile((1.0 - pval)[None, :], (128, 1)))
        vp.add("hasprev", np.full((128, 1), 1.0 if sgm > 0 else 0.0, np.float32))
        vp.add("eps_gn", np.full((128, 1), GN_EPS, np.float32))
        mu = np.asarray(inp["rwkv_mu"][0], np.float32)
        for nm, o in (("mu_r", 0), ("mu_k", D), ("mu_v", 2 * D), ("mu_z", 3 * D)):
            vp.add(nm, _pv(mu[o:o + D]))
        for nm, o in (("mu_dw", 4 * D), ("mu_da", 4 * D + 96)):
            t_ = np.zeros((128, 1), np.float32); t_[:96, 0] = mu[o:o + 96]
            vp.add(nm, t_)
        for nm, key in (("w0", "rwkv_w0"), ("a0", "rwkv_a0"), ("k_k", "rwkv_k_k"), ("k_a", "rwkv_k_a"), ("r_k", "rwkv_r_k"), ("gn_g", "rwkv_gn_g"), ("gn_b", "rwkv_gn_b")):
            vp.add(nm, _pv(np.asarray(inp[key][0], np.float32).reshape(-1)))
        vec_off = vp.off
        pos = np.zeros((NT,), np.int32)
        pos[HALO:] = inp["positions"][b, t0:t0 + OWN]
        if sgm > 0:
            pos[:HALO] = inp["positions"][b, t0 - HALO:t0]
        qi = np.arange(128)[:, None]; kj = np.arange(128)[None, :]
        NEG = -30000.0
        std = np.concatenate([np.where(kj > qi, 0.0, NEG), np.where(kj <= qi, 0.0, NEG)], axis=1).astype(np.float32)
        nop = std.copy(); nop[:, :128] = NEG
        am = np.stack([std, nop, (nop if sgm == 0 else std)], axis=1)
        m = {"wlora": np.ascontiguousarray(np.stack([inp["rwkv_w_lora"][0], inp["rwkv_a_lora"][0]], axis=0)), "pos": pos, "amask": np.ascontiguousarray(am), "xT": np.ascontiguousarray(xs.T), "wst": wst, "vecs": vp.build(), "cmat": _consts(),
             "sgw": np.ascontiguousarray(inp["sg_w_spatial"].transpose(0, 3, 1, 2)).reshape(2, 128, 16 * 128),
             "sgb": np.ascontiguousarray(inp["sg_b_spatial"].reshape(2, 16 * 128))}
        per_core.append(m)
    return per_core, vec_off, wst.shape[0]


_CACHE = {}


def run(inp, nlayers=N_LAYERS, dbg=False):
    inp = {k: np.asarray(v) for k, v in inp.items()}
    in_maps, vec_off, nslots = prepare(inp, nlayers)
    key = (nlayers, dbg)
    if key not in _CACHE:
        _CACHE[key] = Builder(nlayers, nslots, vec_off, in_maps[0]["vecs"].shape[1], dbg).build()
    nc = _CACHE[key]
    res = run_bass_kernel_spmd(nc, in_maps, core_ids=list(range(NCORES)))
    out = np.zeros((2, 4096, D), np.float32)
    dbgs = []
    for cid in range(NCORES):
        b, sgm = cid // 4, cid % 4
        out[b, sgm * OWN:(sgm + 1) * OWN] = res.results[cid]["outT"].T
        if dbg:
            dbgs.append(res.results[cid]["dbgT"].T)
    return (out, dbgs) if dbg else out


def kernel(**inputs):
    return run(inputs)
```
